# Optimizing a Trainium2 kernel written in Bass

```python
import jax, jax.numpy as jnp
from jax import lax
import numpy as np

D_MODEL = 2048
BATCH = 1
SEQ = 8192
DEPTH = 1

CTX_LEN = 256
GRID_W = 64
D_MIX = D_MODEL
W_CONV = D_MIX // 2
W_LRU = D_MIX - W_CONV
N_CONV_GROUPS = 16
N_LRU_HEADS = 16
LRU_HEAD_DIM = W_LRU // N_LRU_HEADS
CONV_A_WIDTH = 3
CONV_A_LEFT = 1
CONV_B_WIDTH = 4
CONV_B_LEFT = 2
LRU_C = 8.0
N_DIR = 2
D_IN_PROJ = 4 * W_CONV + 2 * W_LRU
EPS = 1e-6

kernel_name = "hybrid_conv_rglru_parallel_heads_dit"


def _rmsnorm(x, g):
    xf = x.astype(jnp.float32)
    y = xf * lax.rsqrt(jnp.mean(xf * xf, axis=-1, keepdims=True) + EPS)
    return (y * g.astype(jnp.float32)).astype(x.dtype)


def _dwconv(x, w, axis, left):
    k = w.shape[0]
    n = x.shape[axis]
    pad = [(0, 0)] * x.ndim
    pad[axis] = (left, k - 1 - left)
    xp = jnp.pad(x, pad)
    out = lax.slice_in_dim(xp, 0, n, axis=axis) * w[0]
    for j in range(1, k):
        out = out + lax.slice_in_dim(xp, j, j + n, axis=axis) * w[j]
    return out


def _conv_latent(x, w, left):
    b, l, ch = x.shape
    rows = l // GRID_W
    return _dwconv(x.reshape(b, rows, GRID_W, ch), w, 2, left).reshape(b, l, ch)


def _conv_context(x, w, left):
    return _dwconv(x, w, 1, left)


def _blockdiag(x, w, b):
    bsz, l, _ = x.shape
    y = jnp.einsum('blhi,hij->blhj', x.reshape(bsz, l, N_LRU_HEADS, LRU_HEAD_DIM), w)
    return y.reshape(bsz, l, W_LRU) + b


def _lru_coeffs(xb, wa, ba, wx, bx, lam):
    xf = xb.astype(jnp.float32)
    r = jax.nn.sigmoid(_blockdiag(xf, wa.astype(jnp.float32), ba.astype(jnp.float32)))
    i = jax.nn.sigmoid(_blockdiag(xf, wx.astype(jnp.float32), bx.astype(jnp.float32)))
    log_a = -LRU_C * r * jax.nn.softplus(-lam.astype(jnp.float32))
    a = jnp.exp(log_a)
    bterm = jnp.sqrt(-jnp.expm1(2.0 * log_a)) * (i * xf)
    return a, bterm


def _combine(e1, e2):
    a1, b1 = e1
    a2, b2 = e2
    return a1 * a2, a2 * b1 + b2


def _linear_scan(a, b, h0, reverse):
    if h0 is not None:
        idx = -1 if reverse else 0
        b = b.at[:, idx].add(a[:, idx] * h0)
    _, h = lax.associative_scan(_combine, (a, b), reverse=reverse, axis=1)
    return h


def _split_proj(p):
    cuts = [W_CONV, 2 * W_CONV, 3 * W_CONV, 4 * W_CONV, 4 * W_CONV + W_LRU]
    return jnp.split(p, cuts, axis=-1)


def setup_inputs(seed: int = 0) -> dict:
    key = jax.random.key(seed)
    ks = jax.random.split(key, 20)
    f = jnp.float32
    nrm = lambda k, s, sc: jax.random.normal(k, s, f) * sc
    a0 = jax.random.uniform(ks[17], (DEPTH, N_DIR, W_LRU), f, 0.9, 0.999)
    s = a0 ** (1.0 / LRU_C)
    lru_lambda = jnp.log(s) - jnp.log1p(-s)
    return {
        "x": nrm(ks[0], (BATCH, SEQ, D_MODEL), 1.0),
        "c": nrm(ks[1], (BATCH, D_MODEL), 1.0),
        "ctx": nrm(ks[2], (BATCH, CTX_LEN, D_MODEL), 1.0),
        "c_ctx": nrm(ks[3], (D_MODEL,), 1.0),
        "norm_g": 1.0 + nrm(ks[4], (DEPTH, D_MODEL), 0.02),
        "w_ada": nrm(ks[5], (DEPTH, D_MODEL, 3 * D_MODEL), 0.5 * D_MODEL ** -0.5),
        "b_ada": nrm(ks[6], (DEPTH, 3 * D_MODEL), 0.02),
        "w_in": nrm(ks[7], (DEPTH, D_MODEL, D_IN_PROJ), D_MODEL ** -0.5),
        "w_conv_a": nrm(ks[8], (DEPTH, CONV_A_WIDTH, W_CONV), CONV_A_WIDTH ** -0.5),
        "w_conv_b": nrm(ks[9], (DEPTH, CONV_B_WIDTH, W_LRU), CONV_B_WIDTH ** -0.5),
        "b_conv_b": nrm(ks[10], (DEPTH, W_LRU), 0.02),
        "lru_wa": nrm(ks[11], (DEPTH, N_DIR, N_LRU_HEADS, LRU_HEAD_DIM, LRU_HEAD_DIM), LRU_HEAD_DIM ** -0.5),
        "lru_ba": nrm(ks[12], (DEPTH, N_DIR, W_LRU), 0.02),
        "lru_wx": nrm(ks[13], (DEPTH, N_DIR, N_LRU_HEADS, LRU_HEAD_DIM, LRU_HEAD_DIM), LRU_HEAD_DIM ** -0.5),
        "lru_bx": nrm(ks[14], (DEPTH, N_DIR, W_LRU), 0.02),
        "lru_lambda": lru_lambda,
        "w_out": nrm(ks[15], (DEPTH, D_MIX, D_MODEL), D_MIX ** -0.5),
        "final_g": 1.0 + nrm(ks[16], (D_MODEL,), 0.02),
    }


def reference(x, c, ctx, c_ctx, norm_g, w_ada, b_ada, w_in, w_conv_a, w_conv_b, b_conv_b,
              lru_wa, lru_ba, lru_wx, lru_bx, lru_lambda, w_out, final_g):
    h_lat = x
    h_ctx = ctx
    for l in range(DEPTH):
        last = l == DEPTH - 1
        mod_lat = jax.nn.silu(c) @ w_ada[l] + b_ada[l]
        sh_l, sc_l, gt_l = jnp.split(mod_lat, 3, axis=-1)
        mod_ctx = jax.nn.silu(c_ctx) @ w_ada[l] + b_ada[l]
        sh_c, sc_c, gt_c = jnp.split(mod_ctx, 3, axis=-1)

        hl = _rmsnorm(h_lat, norm_g[l]) * (1.0 + sc_l[:, None]) + sh_l[:, None]
        hc = _rmsnorm(h_ctx, norm_g[l]) * (1.0 + sc_c) + sh_c

        bl, cl, ul, gl, vl, ql = _split_proj(hl @ w_in[l])
        bc, cc_, uc, gc, vc, qc = _split_proj(hc @ w_in[l])

        ya = bl * _conv_latent(cl * ul, w_conv_a[l], CONV_A_LEFT) * jax.nn.silu(gl)

        xbl = _conv_latent(vl, w_conv_b[l], CONV_B_LEFT) + b_conv_b[l]
        xbc = _conv_context(vc, w_conv_b[l], CONV_B_LEFT) + b_conv_b[l]
        y_lru = None
        ctx_states = []
        for d, rev in enumerate((False, True)):
            a_c, b_c = _lru_coeffs(xbc, lru_wa[l, d], lru_ba[l, d], lru_wx[l, d], lru_bx[l, d], lru_lambda[l, d])
            hs_c = _linear_scan(a_c, b_c, None, rev)
            h0 = hs_c[:, 0] if rev else hs_c[:, -1]
            a_l, b_l = _lru_coeffs(xbl, lru_wa[l, d], lru_ba[l, d], lru_wx[l, d], lru_bx[l, d], lru_lambda[l, d])
            hs_l = _linear_scan(a_l, b_l, h0, rev)
            y_lru = hs_l if y_lru is None else y_lru + hs_l
            ctx_states.append(hs_c)
        yb = y_lru.astype(h_lat.dtype) * jax.nn.silu(ql)

        out_lat = jnp.concatenate([ya, yb], axis=-1) @ w_out[l]
        new_lat = h_lat + gt_l[:, None] * out_lat

        if not last:
            ya_c = bc * _conv_context(cc_ * uc, w_conv_a[l], CONV_A_LEFT) * jax.nn.silu(gc)
            yb_c = (ctx_states[0] + ctx_states[1]).astype(h_ctx.dtype) * jax.nn.silu(qc)
            out_ctx = jnp.concatenate([ya_c, yb_c], axis=-1) @ w_out[l]
            h_ctx = h_ctx + gt_c * out_ctx
        h_lat = new_lat
    return _rmsnorm(h_lat, final_g)
```

```python
from contextlib import ExitStack
import numpy as np
import concourse.bass as bass
import concourse.mybir as mybir
from concourse.bass_utils import run_bass_kernel_spmd

F32 = mybir.dt.float32
BF16 = mybir.dt.bfloat16
AF = mybir.ActivationFunctionType
ALU = mybir.AluOpType

NCORES = 8
D = 2048
TOK = 1024
CTX = 256
NSEG = 9
EPS = 1e-6


class Buf:
    def __init__(self, name):
        self.name = name
        self.w = None
        self.r = []


class Eng:
    def __init__(self, nc, es, handle, name):
        self.h = handle
        self.sem = es.enter_context(nc.semaphore("sem_" + name))
        self.n = 0
        self.seen = {}

    def wait(self, ev):
        sem, cnt = ev
        key = id(sem)
        if sem is self.sem:
            pass
        if self.seen.get(key, 0) >= cnt:
            return
        self.seen[key] = cnt
        self.h.wait_ge(sem, cnt)


class K:
    def __init__(self, nc, es):
        self.nc = nc
        self.es = es
        self.pe = Eng(nc, es, nc.tensor, "pe")
        self.act = Eng(nc, es, nc.scalar, "act")
        self.dve = Eng(nc, es, nc.vector, "dve")
        self.pool = Eng(nc, es, nc.gpsimd, "pool")
        self.sp = Eng(nc, es, nc.sync, "sp")
        self.dsems = {}

    def _deps(self, eng, reads, writes):
        evs = []
        for b in reads:
            if b.w is not None:
                evs.append(b.w)
        for b in writes:
            if b.w is not None:
                evs.append(b.w)
            evs.extend(b.r)
        for ev in evs:
            eng.wait(ev)

    def op(self, eng, fn, reads=(), writes=(), signal=True, same_pe_ok=False):
        self._deps(eng, reads, writes)
        ins = fn(eng.h)
        if signal:
            eng.n += 1
            ins.then_inc(eng.sem, 1)
            ev = (eng.sem, eng.n)
            eng.seen[id(eng.sem)] = max(eng.seen.get(id(eng.sem), 0), 0)
        else:
            ev = None
        return ev

    def commit(self, ev, reads=(), writes=()):
        for b in writes:
            b.w = ev
            b.r = []
        for b in reads:
            b.r.append(ev)

    def do(self, eng, fn, reads=(), writes=()):
        ev = self.op(eng, fn, reads, writes)
        self.commit(ev, reads, writes)
        return ev

    def dma(self, eng, out, in_, reads=(), writes=(), key=None, **kw):
        key = key or (writes[0].name if writes else reads[0].name)
        if key not in self.dsems:
            self.dsems[key] = [self.es.enter_context(self.nc.semaphore("d_" + key)), 0]
        ds = self.dsems[key]
        self._deps(eng, reads, writes)
        ds[1] += 16
        eng.h.dma_start(out=out, in_=in_, **kw).then_inc(ds[0], 16)
        ev = (ds[0], ds[1])
        self.commit(ev, reads, writes)
        return ev


def build_program():
    nc = bass.Bass("TRN2", target_bir_lowering=False)
    di = lambda n, s: nc.dram_tensor(n, s, F32, kind="ExternalInput").ap()
    x_own = di("x_own", [TOK, D])
    x_out = di("x_out", [7, TOK, D])
    ctx2 = di("ctx2", [2, CTX, D])
    svec = di("svec", [128, 16, 2])
    gvec = di("gvec", [128, 16])
    w_ada = di("w_ada", [D, 3 * D])
    b_ada = di("b_ada", [128, 48])
    w_in = di("w_in", [D, 6144])
    w_out = di("w_out", [D, D])
    seg_wa = di("seg_wa", [NSEG + 2, 16, 64, 64])
    seg_wx = di("seg_wx", [NSEG + 2, 16, 64, 64])
    seg_vec = di("seg_vec", [NSEG + 2, 128, 3, 8])
    seg_w5 = di("seg_w5", [NSEG + 2, 128, 8, 5])
    bconv = di("bconv", [128, 8])
    wca = di("wca", [128, 8, 3])
    fgbc = di("fgbc", [128, D])
    onehot = di("onehot", [128, 8])
    eye = di("eye", [128, 128])
    out_d = nc.dram_tensor("out", [TOK, D], F32, kind="ExternalOutput").ap()

    with ExitStack() as es:
        k = K(nc, es)
        PE, ACT, DVE, POOL, SP = k.pe, k.act, k.dve, k.pool, k.sp
        sb = lambda n, s, dt=F32: es.enter_context(nc.sbuf_tensor(n, s, dt))
        pst = lambda n, s, dt=F32: es.enter_context(nc.psum_tensor(n, s, dt))

        big = sb("big", [128, 16384], F32)
        wv = big[:, 0:8192].bitcast(BF16).rearrange("p (k c) -> p k c", k=16)
        xnT = big[:, 8192:16384].bitcast(BF16).rearrange("p (k t) -> p k t", k=16)
        xnTh = [big[:, 8192 + 4096 * i:8192 + 4096 * (i + 1)].bitcast(BF16).rearrange("p (k t) -> p k t", k=16)
                for i in range(2)]
        newlat = big[:, :].rearrange("p (t d) -> p t d", t=8)
        B_wv = Buf("wv")
        B_xh = [Buf("xnT0"), Buf("xnT1")]
        yy = sb("yy", [128, 16, 1024], BF16)
        B_y = Buf("yy")
        yflat = yy[:].rearrange("p k t -> p (k t)")
        stg = yflat.bitcast(F32)[:, 0:4096].rearrange("p (k c) -> p k c", k=16)
        raw = yflat[:, 8192:12288].rearrange("p (k c) -> p k c", k=16)
        xt = [sb(f"xt{i}", [128, D]) for i in range(2)]
        B_xt = [Buf(f"xt{i}") for i in range(2)]
        xnbs = [sb(f"xnb{i}", [128, D], BF16) for i in range(2)]
        B_xnbs = [Buf(f"xnb{i}") for i in range(2)]
        wst = [sb(f"wst{i}", [128, 16, 256], BF16) for i in range(2)]
        B_wst = [Buf(f"wst{i}") for i in range(2)]
        NT = 9
        tmp = [sb(f"tmp{i}", [128, 1024]) for i in range(NT)]
        B_tmp = [Buf(f"tmp{i}") for i in range(NT)]
        xbb = sb("xbb", [128, 2048], BF16)
        B_xbb = Buf("xbb")
        NBD = 3
        bd = [sb(f"bd{i}", [128, 2, 8, 128], BF16) for i in range(NBD)]
        B_bd = [Buf(f"bd{i}") for i in range(NBD)]
        bdo = sb("bdo", [128, 4, 8, 128], BF16)
        B_bdo = Buf("bdo")
        identb = sb("identb", [128, 128], BF16)
        identf = sb("identf", [128, 128])
        B_id = Buf("ident")
        small = sb("small", [128, 1024])
        B_sm = Buf("small")
        sv = small[:, 0:32].rearrange("p (k v) -> p k v", v=2)
        svb = sb("svb", [128, 16, 2], BF16)
        gv = small[:, 32:48]
        modv = small[:, 48:144].rearrange("p (v c) -> p v c", v=2)
        gam = small[:, 144:176].rearrange("p (v k) -> p v k", v=2)
        shb = sb("shb", [128, 16, 2], BF16)
        bv = small[:, 176:192].rearrange("p (j v) -> p j v", v=2)
        B_bv = Buf("bv")
        bcv = small[:, 192:200]
        wcav = small[:, 200:224].rearrange("p (j t) -> p j t", t=3)
        ohv = small[:, 224:232]
        B_cst = Buf("cst")
        gtv = small[:, 232:248]
        ownv = small[:, 256:304].rearrange("p (d a j) -> p d a j", d=2, a=3)
        ownc = small[:, 304:336].rearrange("p (d a j) -> p d a j", d=2, a=2)
        owne = small[:, 336:352].rearrange("p (d j) -> p d j", d=2)
        ownh = small[:, 352:384].rearrange("p (d a j) -> p d a j", d=2, a=2)
        w5o = small[:, 384:424].rearrange("p (j t) -> p j t", t=5)
        NR = 3
        segv = [small[:, 448 + 24 * i:448 + 24 * (i + 1)].rearrange("p (a j) -> p a j", a=3) for i in range(NR)]
        segh = [small[:, 520 + 16 * i:520 + 16 * (i + 1)].rearrange("p (a j) -> p a j", a=2) for i in range(NR)]
        segc = [small[:, 568 + 16 * i:568 + 16 * (i + 1)].rearrange("p (a j) -> p a j", a=2) for i in range(NR)]
        sege = [small[:, 616 + 8 * i:616 + 8 * (i + 1)] for i in range(NR)]
        w5r = [small[:, 640 + 40 * i:640 + 40 * (i + 1)].rearrange("p (j t) -> p j t", t=5) for i in range(NR)]
        B_seg = [Buf(f"seg{i}") for i in range(NR)]
        segcb = [small[:, 896 + 8 * i:896 + 8 * (i + 1)] for i in range(NR)]
        NST = 4
        ST = [small[:, 768 + 8 * i:768 + 8 * (i + 1)] for i in range(NST)]
        B_ST = [Buf(f"ST{i}") for i in range(NST)]
        SI = [small[:, 800 + 8 * i:800 + 8 * (i + 1)] for i in range(2)]
        B_SI = [Buf(f"SI{i}") for i in range(2)]
        cf = small[:, 816:824]; B_cf = Buf("cf")
        ctxr = small[:, 824:832]; B_ctxr = Buf("ctxr")
        Dt = small[:, 832:840]; B_Dt = Buf("Dt")
        zero8 = small[:, 840:848]; B_z8 = Buf("z8")
        NQ = 4
        ssq = [small[:, 856 + i:857 + i] for i in range(NQ)]
        rstd = [small[:, 864 + i:865 + i] for i in range(NQ)]
        B_q = [Buf(f"q{i}") for i in range(NQ)]
        cst = small[:, 880:884]
        B_eps = Buf("eps")
        EPS_AP, ONE_AP, NHALF_AP = cst[:, 0:1], cst[:, 1:2], cst[:, 2:3]

        psA = pst("psA", [128, 1024])
        psB = pst("psB", [128, 1024])
        psC = pst("psC", [128, 1024])
        psD = pst("psD", [128, 1024])
        B_ps = {n: Buf(n) for n in "ABCD"}
        PSM = {"A": psA, "B": psB, "C": psC, "D": psD}
        psV = [psA[:, 0:512], psA[:, 512:1024]]
        psGa = [psB[:, 0:512], psB[:, 512:1024]]
        psGx = [psC[:, 0:512], psC[:, 512:1024]]
        B_pV = [Buf("pV0"), Buf("pV1")]
        B_pGa = [Buf("pGa0"), Buf("pGa1")]
        B_pGx = [Buf("pGx0"), Buf("pGx1")]

        k.do(POOL, lambda h: h.memset(cst[:, 0:1], EPS), writes=[B_eps])
        k.do(POOL, lambda h: h.memset(cst[:, 1:2], 1.0), writes=[B_eps])
        k.do(POOL, lambda h: h.memset(cst[:, 2:3], -0.5), writes=[B_eps])
        k.do(POOL, lambda h: h.memset(zero8, 0.0), writes=[B_z8])
        k.dma(SP, identf[:], eye, writes=[B_id], key="idf")
        k.dma(POOL, identb[:], eye, writes=[B_id], key="c1")
        k.dma(SP, sv, svec, writes=[B_sm], key="c0")
        k.dma(SP, gv, gvec, writes=[B_sm], key="c0")
        k.dma(SP, bcv, bconv, writes=[B_cst], key="c2")
        k.dma(SP, wcav, wca, writes=[B_cst], key="c2")
        k.dma(SP, ohv, onehot, writes=[B_cst], key="c2")
        k.dma(SP, modv[:, 0, :], b_ada, writes=[B_sm], key="c0")
        k.dma(SP, modv[:, 1, :], b_ada, writes=[B_sm], key="c0")
        for i in range(NBD):
            k.do(POOL, lambda h, i=i: h.memset(bd[i][:], 0.0), writes=[B_bd[i]])
        k.do(POOL, lambda h: h.memset(bdo[:], 0.0), writes=[B_bdo])
        k.do(ACT, lambda h: h.activation(out=tmp[0][:, 0:32], in_=small[:, 0:32], func=AF.Sigmoid),
             reads=[B_sm], writes=[B_tmp[0]])
        k.do(DVE, lambda h: h.tensor_tensor(out=small[:, 0:32], in0=small[:, 0:32], in1=tmp[0][:, 0:32],
                                            op=ALU.mult), reads=[B_tmp[0]], writes=[B_sm])
        B_svb = Buf("svb")
        k.do(DVE, lambda h: h.tensor_copy(out=svb[:], in_=sv), reads=[B_sm], writes=[B_svb])

        win_v = w_in.rearrange("(kc p) c -> p kc c", p=128)
        wout_v = w_out.rearrange("(kc p) c -> p kc c", p=128)
        wada_v = w_ada.rearrange("(kc p) c -> p kc c", p=128)
        wcount = [0]

        xtw = [xt[i][:].bitcast(BF16).rearrange("p (k c) -> p k c", k=16) for i in range(2)]
        WB = [(wst[0], B_wst[0], "wst0"), (wst[1], B_wst[1], "wst1"), (xtw[0], B_xt[0], "wxt0"), (xtw[1], B_xt[1], "wxt1")]

        def wload(src_view, c0, ncols=256):
            i = wcount[0] % len(WB)
            wcount[0] += 1
            k.dma(POOL, WB[i][0][:, :, 0:ncols], src_view[:, :, c0:c0 + ncols], writes=[WB[i][1]], key=WB[i][2])
            return i

        def pe_group(emit, reads, writes):
            k._deps(PE, reads, writes)
            ins = emit(PE.h)
            PE.n += 1
            ins.then_inc(PE.sem, 1)
            k.commit((PE.sem, PE.n), reads, writes)

        def mm_group(pn, ncols, lhs_fn, rhs_fn, reads, nk=16, out_ap=None, B_out=None):
            halves = [(h0, min(512, ncols - h0)) for h0 in range(0, ncols, 512)]
            dst = PSM[pn] if out_ap is None else out_ap
            Bo = B_ps[pn] if B_out is None else B_out

            def emit(h):
                ins = None
                for kc in range(nk):
                    for (h0, hn) in halves:
                        ins = h.matmul(dst[:, h0:h0 + hn], lhsT=lhs_fn(kc), rhs=rhs_fn(kc, h0, hn),
                                       start=(kc == 0), stop=(kc == nk - 1))
                return ins
            pe_group(emit, reads, [Bo])

        def mod_cols(col_list, pp="CD"):
            blocks = sorted(set(c // 2 for c in col_list))
            depth = max(1, len(WB) - 1)
            pend = []
            nxt_ = 0
            for bi, blk in enumerate(blocks):
                while nxt_ < len(blocks) and len(pend) < depth:
                    pend.append(wload(wada_v, blocks[nxt_] * 256))
                    nxt_ += 1
                cur = pend.pop(0)
                for cc in range(2):
                    col = blk * 2 + cc
                    pn = pp[col % 2]
                    mm_group(pn, 2, lambda kc, cc=cc, cur=cur: WB[cur][0][:, kc, cc * 128:(cc + 1) * 128],
                             lambda kc, h0, hn: svb[:, kc, :], [WB[cur][1], B_svb])
                    k.do(DVE, lambda h, col=col, pn=pn: h.tensor_tensor(
                        out=modv[:, :, col], in0=PSM[pn][:, 0:2], in1=modv[:, :, col], op=ALU.add),
                        reads=[B_ps[pn]], writes=[B_sm])

        B_wvp = [Buf(f"wvp{i}") for i in range(4)]

        def load_wv_raw():
            for piece in range(4):
                c0 = 4096 + piece * 256
                k.dma(POOL, wv[:, :, piece * 256:(piece + 1) * 256], win_v[:, :, c0:c0 + 256],
                      reads=[], writes=[B_wvp[piece]], key=f"wvp{piece}")

        load_wv_raw()
        def mod_block(cur, blk, pp="CD"):
            for cc in range(2):
                col = blk * 2 + cc
                pn = pp[col % 2]
                mm_group(pn, 2, lambda kc, cc=cc, cur=cur: WB[cur][0][:, kc, cc * 128:(cc + 1) * 128],
                         lambda kc, h0, hn: svb[:, kc, :], [WB[cur][1], B_svb])
                k.do(DVE, lambda h, col=col, pn=pn: h.tensor_tensor(
                    out=modv[:, :, col], in0=PSM[pn][:, 0:2], in1=modv[:, :, col], op=ALU.add),
                    reads=[B_ps[pn]], writes=[B_sm])

        mod_cols(range(16, 32))
        for v in range(2):
            k.do(DVE, lambda h, v=v: h.scalar_tensor_tensor(
                out=gam[:, v, :], in0=modv[:, v, 16:32], scalar=1.0, in1=gv, op0=ALU.add, op1=ALU.mult),
                reads=[B_sm], writes=[B_cst])
        mod_cols(range(0, 16))
        B_shb = Buf("shb")
        for v in range(2):
            k.do(DVE, lambda h, v=v: h.tensor_copy(out=shb[:, :, v], in_=modv[:, v, 0:16]), reads=[B_sm], writes=[B_shb])

        stg2 = [yflat.bitcast(F32)[:, 4096 * i:4096 * (i + 1)].rearrange("p (k c) -> p k c", k=16) for i in range(2)]
        B_stg = [Buf("stg0"), Buf("stg1")]
        sgc = [0]

        def wv_bias():
            for j in range(8):
                pn = "CD"[j % 2]
                mm_group(pn, 2, lambda kc, j=j: wv[:, kc, j * 128:(j + 1) * 128], lambda kc, h0, hn: shb[:, kc, :],
                         [B_wv, B_shb] + B_wvp)
                k.do(DVE, lambda h, j=j, pn=pn: h.tensor_copy(out=bv[:, j, :], in_=PSM[pn][:, 0:2]),
                     reads=[B_ps[pn]], writes=[B_bv])

        def wv_rescale(v):
            for piece in range(4):
                wp = wv[:, :, piece * 256:(piece + 1) * 256]
                k.do(DVE, lambda h, wp=wp, v=v: h.tensor_tensor(
                    out=wp, in0=wp, in1=gam[:, v, :].unsqueeze(2).broadcast_to([128, 16, 256]), op=ALU.mult),
                    reads=[B_wv, B_cst] + B_wvp, writes=[B_wv])

        gbc = yflat.bitcast(F32)[:, 0:2048]
        B_gbc = Buf("gbc")

        def build_gbc():
            ones_f = tmp[7][:, 0:128]
            k.do(POOL, lambda h: h.memset(ones_f, 1.0), writes=[B_tmp[7]])
            for c in range(16):
                dg = tmp[7][:, 128 + 128 * (c % 2):256 + 128 * (c % 2)]
                pn = "CD"[c % 2]
                k.do(DVE, lambda h, dg=dg, c=c: h.tensor_scalar(out=dg, in0=identf[:], scalar1=gam[:, 1, c:c + 1],
                                                                scalar2=None, op0=ALU.mult),
                     reads=[B_id, B_cst], writes=[B_tmp[7]])
                pe_group(lambda h, dg=dg, pn=pn: h.matmul(PSM[pn][:, 0:128], lhsT=ones_f, rhs=dg, start=True, stop=True),
                         [B_tmp[7]], [B_ps[pn]])
                k.do(DVE, lambda h, pn=pn, c=c: h.tensor_copy(out=gbc[:, c * 128:(c + 1) * 128], in_=PSM[pn][:, 0:128]),
                     reads=[B_ps[pn]], writes=[B_gbc])

        B_hlodd = Buf("hlodd")
        xcount = [0]
        qcount = [0]
        tcount = [0]

        def t0_tile(src, T_off, dst, B_dst, modulate, defer=False, pn="D", mi=0):
            Bd_ = B_dst if isinstance(B_dst, list) else [B_dst]
            i = xcount[0] % 2
            xcount[0] += 1
            q = qcount[0] % NQ
            qcount[0] += 1
            xnb, B_xnb = xnbs[i], B_xnbs[i]
            k.dma(SP, xt[i][:], src, writes=[B_xt[i]], key=f"xt{i}")
            k.do(ACT, lambda h: h.activation(out=tmp[7][:, 0:1024].bitcast(BF16), in_=xt[i][:], func=AF.Square,
                                             accum_out=ssq[q]),
                 reads=[B_xt[i]], writes=[B_tmp[7], B_q[q]])
            if modulate:
                k.do(ACT, lambda h: h.activation(out=rstd[q], in_=ssq[q], func=AF.Sqrt, scale=1.0 / D, bias=EPS_AP),
                     reads=[B_q[q], B_eps], writes=[B_q[q]])
                k.do(DVE, lambda h: h.reciprocal(out=rstd[q], in_=rstd[q]), reads=[B_q[q]], writes=[B_q[q]])
            else:
                k.do(POOL, lambda h: h.tensor_scalar(out=rstd[q], in0=ssq[q], scalar1=1.0 / D, scalar2=EPS, op0=ALU.mult,
                                                     op1=ALU.add), reads=[B_q[q]], writes=[B_q[q]])
                k.do(POOL, lambda h: h.tensor_tensor(out=rstd[q], in0=rstd[q], in1=NHALF_AP, op=ALU.pow),
                     reads=[B_q[q], B_eps], writes=[B_q[q]])
            if mi == 1:
                k.do(DVE, lambda h: h.scalar_tensor_tensor(out=xnb[:], in0=xt[i][:], scalar=rstd[q], in1=gbc,
                                                           op0=ALU.mult, op1=ALU.mult),
                     reads=[B_xt[i], B_q[q], B_gbc], writes=[B_xnb])
            else:
                k.do(DVE, lambda h: h.tensor_scalar(out=xnb[:], in0=xt[i][:], scalar1=rstd[q], scalar2=None,
                                                    op0=ALU.mult), reads=[B_xt[i], B_q[q]], writes=[B_xnb])
            pv = PSM[pn][:, :].bitcast(BF16)[:, 0:2048].rearrange("p (k t) -> p k t", k=16)

            def emit(h):
                for kc in range(16):
                    ins = h.transpose(out=pv[:, kc, :], in_=xnb[:, kc * 128:(kc + 1) * 128], identity=identb[:])
                return ins
            pe_group(emit, [B_xnb, B_id], [B_ps[pn]])

            def phase_b():
                if not modulate:
                    tcount[0] += 1
                    if True:
                        k.do(DVE, lambda h: h.tensor_copy(out=dst[:, :, T_off:T_off + 128], in_=pv),
                             reads=[B_ps[pn]], writes=Bd_)
                    else:
                        k.do(ACT, lambda h: h.activation(out=dst[:, :, T_off:T_off + 128], in_=pv, func=AF.Copy),
                             reads=[B_ps[pn]], writes=Bd_)
                else:
                    k._deps(ACT, [B_ps[pn], B_cst, B_sm], Bd_)
                    k._deps(DVE, [B_ps[pn], B_cst, B_sm], [B_hlodd])
                    for kc in range(0, 8):
                        ins = ACT.h.activation(out=dst[:, kc, T_off:T_off + 128], in_=pv[:, kc, :], func=AF.Identity,
                                               scale=gam[:, 0, kc:kc + 1], bias=modv[:, 0, kc:kc + 1])
                    ACT.n += 1
                    ins.then_inc(ACT.sem, 1)
                    for kc in range(8, 16):
                        ins2 = DVE.h.tensor_scalar(out=dst[:, kc, T_off:T_off + 128], in0=pv[:, kc, :],
                                                   scalar1=gam[:, 0, kc:kc + 1], scalar2=modv[:, 0, kc:kc + 1],
                                                   op0=ALU.mult, op1=ALU.add)
                    DVE.n += 1
                    ins2.then_inc(DVE.sem, 1)
                    k.commit((ACT.sem, ACT.n), [B_ps[pn], B_cst, B_sm], Bd_)
                    k.commit((DVE.sem, DVE.n), [B_ps[pn], B_cst, B_sm], [B_hlodd])
            if defer:
                return phase_b
            phase_b()
            return None

        def seg_consts(si, dst_v, dst_h, dst_c, dst_e, Bd, phase=0):
            if phase in (0, 1):
                k.dma(SP, dst_v, seg_vec[si], writes=[Bd], key="sc_" + Bd.name)
            if phase == 1:
                return
            k.do(ACT, lambda h: h.activation(out=dst_e, in_=dst_v[:, 2, :], func=AF.Exp, scale=-1.0),
                 reads=[Bd], writes=[Bd])
            k.do(ACT, lambda h: h.activation(out=dst_e, in_=dst_e, func=AF.Ln, bias=ONE_AP),
                 reads=[Bd, B_eps], writes=[Bd])
            k.do(DVE, lambda h: h.tensor_scalar(out=dst_c[:, 0, :], in0=dst_e, scalar1=-4.0, scalar2=None, op0=ALU.mult),
                 reads=[Bd], writes=[Bd])
            k.do(DVE, lambda h: h.tensor_scalar(out=dst_c[:, 1, :], in0=dst_e, scalar1=-8.0, scalar2=None, op0=ALU.mult),
                 reads=[Bd], writes=[Bd])
            k.do(DVE, lambda h: h.tensor_scalar(out=dst_h, in0=dst_v[:, 0:2, :], scalar1=0.5, scalar2=None, op0=ALU.mult),
                 reads=[Bd], writes=[Bd])

        BDS = {}

        def bd_subs(B_dst):
            if B_dst.name not in BDS:
                BDS[B_dst.name] = [Buf(f"{B_dst.name}_s{i}") for i in range(8)]
            return BDS[B_dst.name]

        def load_bd(dst4, slot0, si, B_dst):
            subs_ = bd_subs(B_dst)
            k._deps(POOL, [], [B_dst])
            B_dst.r = []
            for m, src in enumerate((seg_wa, seg_wx)):
                s4 = src[si].rearrange("(j two) i o -> two i j o", two=2)
                for two in range(2):
                    sbuf_ = subs_[(slot0 // 2) * 4 + m * 2 + two]
                    k.dma(POOL, dst4[two * 64:(two + 1) * 64, slot0 + m, :, two * 64:(two + 1) * 64], s4[two],
                          writes=[sbuf_], key="bd_" + sbuf_.name)

        def conv5(T, L, j, w5ap, Bw, vsb, xb, Bv, Bx, skip_first=False):
            v3 = lambda ap, a, b: ap.rearrange("p (r l) -> p r l", l=L)[:, :, a:b]
            if not skip_first:
              k.do(DVE, lambda h: h.tensor_scalar(out=xb, in0=vsb, scalar1=w5ap[:, j, 2:3],
                                                scalar2=bcv[:, j:j + 1], op0=ALU.mult, op1=ALU.add),
                 reads=[Bv, Bw, B_cst], writes=[Bx])
            for tap, off in ((1, -1), (0, -2), (3, 1), (4, 2)):
                if off < 0:
                    o_ap, i_ap = v3(xb, -off, L), v3(vsb, 0, L + off)
                else:
                    o_ap, i_ap = v3(xb, 0, L - off), v3(vsb, off, L)
                k.do(DVE, lambda h, o_ap=o_ap, i_ap=i_ap, tap=tap: h.scalar_tensor_tensor(
                    out=o_ap, in0=i_ap, scalar=w5ap[:, j, tap:tap + 1], in1=o_ap, op0=ALU.mult, op1=ALU.add),
                    reads=[Bv, Bw], writes=[Bx])

        half = lambda t, i: t[:, i * 512:(i + 1) * 512]
        V_ = [half(tmp[0], 0), half(tmp[0], 1)]
        H_ = [half(tmp[1], 0), half(tmp[1], 1)]
        XB_ = [half(tmp[2], 0), half(tmp[2], 1), half(tmp[3], 0), half(tmp[3], 1)]
        TI_ = [half(tmp[4], 0), half(tmp[4], 1), half(tmp[5], 0), half(tmp[5], 1)]
        A_ = [half(tmp[6], 0), half(tmp[6], 1), half(tmp[8], 0), half(tmp[8], 1)]
        ytail = yflat.bitcast(F32)[:, 6144:8192]
        TR_ = [ytail[:, i * 512:(i + 1) * 512] for i in range(4)]
        XBB_ = [xbb[:, 512 * i:512 * (i + 1)] for i in range(4)]
        mk = lambda n, c: [Buf(f"{n}{i}") for i in range(c)]
        B_V, B_H, B_XB, B_TI, B_A, B_TR, B_XBB = mk("V", 2), mk("H", 2), mk("XB", 4), mk("TI", 4), mk("A", 4), mk("TR", 4), mk("XBB", 4)
        ALLP = B_V + B_H + B_XB + B_TI + B_A + B_TR + B_XBB

        def run_pipeline(subs):
            units = []
            for ssi, ss in enumerate(subs):
                for j in range(8):
                    units.append((ssi, j))
            NU = len(units)

            def t0_sub(ssi):
                ss = subs[ssi]
                for t in range(ss["T"] // 128):
                    yield (ssi, t)
            t0_list = []
            for ssi in range(len(subs)):
                t0_list.extend(list(t0_sub(ssi)))
            t0_pos = [0]
            pending_b = []

            def flush_t0():
                while pending_b:
                    pending_b.pop(0)()


            def emit_t0(n):
                for _ in range(n):
                    if t0_pos[0] >= len(t0_list):
                        return
                    ssi, t = t0_list[t0_pos[0]]
                    t0_pos[0] += 1
                    ss = subs[ssi]
                    flush_t0()
                    pb = t0_tile(ss["src"][t * 128:(t + 1) * 128, :], t * 128, xnTh[ssi % 2], B_xh[ssi % 2],
                                 modulate=False, defer=True, mi=ss["mi"])
                    pending_b.append(pb)

            def t0_done_for(ssi):
                return sum(subs[i]["T"] // 128 for i in range(ssi + 1))

            def P1(u):
                ssi, j = units[u]
                ss = subs[ssi]
                T = ss["T"]
                need = t0_done_for(ssi)
                if t0_pos[0] < need:
                    emit_t0(need - t0_pos[0])
                if j == 0:
                    flush_t0()
                    if ss.get("pre") is not None:
                        ss["pre"]()
                mm_group(None, T, lambda kc: wv[:, kc, j * 128:(j + 1) * 128],
                         lambda kc, h0, hn: xnTh[ssi % 2][:, kc, h0:h0 + hn], [B_wv, B_xh[ssi % 2]] + B_wvp,
                         out_ap=psV[u % 2], B_out=B_pV[u % 2])

            def S1(u):
                ssi, j = units[u]
                ss = subs[ssi]
                T, L, mi, r = ss["T"], ss["L"], ss["mi"], ss["seg"] % NR
                p2, p4 = u % 2, u % 4
                k.do(ACT, lambda h: h.activation(out=V_[p2][:, 0:T], in_=psV[p2][:, 0:T], func=AF.Identity,
                                                 bias=bv[:, j, mi:mi + 1]),
                     reads=[B_pV[p2], B_bv], writes=[B_V[p2]])
                k.do(ACT, lambda h: h.activation(out=XB_[p4][:, 0:T], in_=psV[p2][:, 0:T], func=AF.Identity,
                                                 scale=w5r[r][:, j, 2:3], bias=segcb[r][:, j:j + 1]),
                     reads=[B_pV[p2], B_seg[r]], writes=[B_XB[p4]])
                conv5(T, L, j, w5r[r], B_seg[r], V_[p2][:, 0:T], XB_[p4][:, 0:T], B_V[p2], B_XB[p4], skip_first=True)
                k.do(POOL, lambda h: h.tensor_copy(out=XBB_[p4][:, 0:T], in_=XB_[p4][:, 0:T]),
                     reads=[B_XB[p4]], writes=[B_XBB[p4]])

            def S2pe(u):
                ssi, j = units[u]
                ss = subs[ssi]
                T, b = ss["T"], ss["seg"] % NBD
                p2, p4 = u % 2, u % 4
                mm_group(None, T, lambda kc: bd[b][:, 0, j, :], lambda kc, h0, hn: XBB_[p4][:, h0:h0 + hn],
                         [B_bd[b], B_XBB[p4]] + bd_subs(B_bd[b]), nk=1, out_ap=psGa[p2], B_out=B_pGa[p2])
                mm_group(None, T, lambda kc: bd[b][:, 1, j, :], lambda kc, h0, hn: XBB_[p4][:, h0:h0 + hn],
                         [B_bd[b], B_XBB[p4]] + bd_subs(B_bd[b]), nk=1, out_ap=psGx[p2], B_out=B_pGx[p2])

            def S2(u):
                ssi, j = units[u]
                ss = subs[ssi]
                T, r = ss["T"], ss["seg"] % NR
                p2, p4 = u % 2, u % 4
                k.do(ACT, lambda h: h.activation(out=TR_[p4][:, 0:T], in_=psGa[p2][:, 0:T], func=AF.Tanh, scale=0.5,
                                                 bias=segh[r][:, 0, j:j + 1]),
                     reads=[B_pGa[p2], B_seg[r]], writes=[B_TR[p4]])
                k.do(ACT, lambda h: h.activation(out=TI_[p4][:, 0:T], in_=psGx[p2][:, 0:T], func=AF.Tanh, scale=0.5,
                                                 bias=segh[r][:, 1, j:j + 1]),
                     reads=[B_pGx[p2], B_seg[r]], writes=[B_TI[p4]])
                k.do(ACT, lambda h: h.activation(out=A_[p4][:, 0:T], in_=TR_[p4][:, 0:T], func=AF.Exp,
                                                 scale=segc[r][:, 0, j:j + 1], bias=segc[r][:, 0, j:j + 1]),
                     reads=[B_TR[p4], B_seg[r]], writes=[B_A[p4]])
                k.do(ACT, lambda h: h.activation(out=TR_[p4][:, 0:T], in_=TR_[p4][:, 0:T], func=AF.Exp,
                                                 scale=segc[r][:, 1, j:j + 1], bias=segc[r][:, 1, j:j + 1]),
                     reads=[B_TR[p4], B_seg[r]], writes=[B_TR[p4]])
                k.do(DVE, lambda h: h.scalar_tensor_tensor(out=TI_[p4][:, 0:T], in0=TI_[p4][:, 0:T], scalar=1.0,
                                                           in1=XB_[p4][:, 0:T], op0=ALU.add, op1=ALU.mult),
                     reads=[B_TI[p4], B_XB[p4]], writes=[B_TI[p4]])

            def SQ(u):
                ssi, j = units[u]
                T = subs[ssi]["T"]
                p4 = u % 4
                k.do(ACT, lambda h: h.activation(out=TR_[p4][:, 0:T], in_=TR_[p4][:, 0:T], func=AF.Sqrt, scale=-1.0,
                                                 bias=ONE_AP), reads=[B_TR[p4], B_eps], writes=[B_TR[p4]])

            def D2(u):
                ssi, j = units[u]
                ss = subs[ssi]
                T = ss["T"]
                p2, p4 = u % 2, u % 4
                if j == 0 and ss.get("hook") is not None:
                    ss["hook"]()
                init_ap, B_init = ss["init"]
                fin, B_fin = ST[ss["fin"]], B_ST[ss["fin"]]
                k.do(DVE, lambda h: h.scalar_tensor_tensor(out=TI_[p4][:, 0:T], in0=TI_[p4][:, 0:T], scalar=0.5,
                                                           in1=TR_[p4][:, 0:T], op0=ALU.mult, op1=ALU.mult),
                     reads=[B_TI[p4], B_TR[p4]], writes=[B_TI[p4]])
                k.do(DVE, lambda h: h.tensor_tensor_scan(out=H_[p2][:, 0:T], data0=A_[p4][:, 0:T], data1=TI_[p4][:, 0:T],
                                                         initial=init_ap[:, j:j + 1], op0=ALU.mult, op1=ALU.add),
                     reads=[B_A[p4], B_TI[p4], B_init], writes=[B_H[p2]])
                k.do(DVE, lambda h: h.tensor_copy(out=fin[:, j:j + 1], in_=H_[p2][:, T - 1:T]),
                     reads=[B_H[p2]], writes=[B_fin])

            seg_dma_done = set()
            seg_cmp_done = set()

            def seg_dma(seg_id):
                if seg_id is None or seg_id in seg_dma_done:
                    return
                seg_dma_done.add(seg_id)
                r, b = seg_id % NR, seg_id % NBD
                load_bd(bd[b], 0, seg_id, B_bd[b])
                seg_consts(seg_id, segv[r], segh[r], segc[r], sege[r], B_seg[r], phase=1)
                k.dma(SP, w5r[r], seg_w5[seg_id], writes=[B_seg[r]], key="sc_" + B_seg[r].name)

            def seg_cmp(seg_id):
                if seg_id is None or seg_id in seg_cmp_done:
                    return
                seg_dma(seg_id)
                seg_cmp_done.add(seg_id)
                r = seg_id % NR
                seg_consts(seg_id, segv[r], segh[r], segc[r], sege[r], B_seg[r], phase=2)
                mi_ = 1 if seg_id < 2 else 0
                k.do(DVE, lambda h: h.tensor_tensor(out=segcb[r], in0=w5r[r][:, :, 2], in1=bv[:, :, mi_], op=ALU.mult),
                     reads=[B_seg[r], B_bv], writes=[B_seg[r]])
                k.do(DVE, lambda h: h.tensor_tensor(out=segcb[r], in0=segcb[r], in1=bcv, op=ALU.add),
                     reads=[B_seg[r], B_cst], writes=[B_seg[r]])

            def seg_prep(seg_id):
                seg_cmp(seg_id)

            seg_prep(subs[0]["seg"])
            for tick in range(NU + 8):
                if 0 <= tick - 3 < NU:
                    S2pe(tick - 3)
                if tick < NU:
                    ssi, j = units[tick]
                    if j == 0:
                        nxt = subs[ssi + 1]["seg"] if ssi + 1 < len(subs) else None
                        nxt2 = subs[ssi + 2]["seg"] if ssi + 2 < len(subs) else None
                        seg_prep(subs[ssi]["seg"])
                        seg_dma(nxt)
                        seg_dma(nxt2)
                        seg_cmp(nxt)
                    P1(tick)
                    if j % 2 == 0 and t0_pos[0] < len(t0_list) and t0_list[t0_pos[0]][0] <= ssi + 1:
                        emit_t0(1)
                if 0 <= tick - 1 < NU:
                    S1(tick - 1)
                u4 = tick - 4
                if u4 >= 0 and u4 % 4 == 3 and u4 < NU:
                    for uu in range(u4 - 3, u4 + 1):
                        SQ(uu)
                    for uu in range(u4 - 3, u4 + 1):
                        D2(uu)
                if 0 <= tick - 3 < NU:
                    S2(tick - 3)
                flush_t0()

        def fence(srcs, dsts):
            evs = []
            for b in srcs:
                if b.w is not None:
                    evs.append(b.w)
                evs.extend(b.r)
            for d_ in dsts:
                d_.r.extend(evs)

        FINE = lambda: ALLP + B_pV + B_pGa + B_pGx
        COARSE = lambda: B_tmp + [B_y, B_xbb] + [B_ps[n_] for n_ in "ABCD"]

        del WB[2:]
        build_gbc()
        wv_bias()

        def boundary(s, prev_idx, dst, B_dst):
            Sp, Bp = ST[prev_idx], B_ST[prev_idx]
            k.do(DVE, lambda h: h.scalar_tensor_tensor(out=cf, in0=Sp, scalar=ohv[:, s:s + 1], in1=cf, op0=ALU.mult,
                                                       op1=ALU.add), reads=[Bp, B_cst, B_cf], writes=[B_cf])
            k.do(DVE, lambda h: h.tensor_tensor(out=Dt, in0=ctxr, in1=Sp, op=ALU.subtract),
                 reads=[B_ctxr, Bp], writes=[B_Dt])
            k.do(DVE, lambda h: h.scalar_tensor_tensor(out=dst, in0=Dt, scalar=ohv[:, s:s + 1], in1=Sp, op0=ALU.mult,
                                                       op1=ALU.add), reads=[B_Dt, B_cst, Bp], writes=[B_dst])

        def first_hook():
            k.do(DVE, lambda h: h.tensor_copy(out=ctxr, in_=ST[1]), reads=[B_ST[1]], writes=[B_ctxr])
            k.do(DVE, lambda h: h.tensor_copy(out=ST[3], in_=ST[0]), reads=[B_ST[0]], writes=[B_ST[3]])
            k.do(POOL, lambda h: h.memset(cf, 0.0), writes=[B_cf])
            boundary(0, 3, SI[0], B_SI[0])

        all_subs = [
            dict(src=ctx2[0], T=CTX, L=CTX, mi=1, seg=0, init=(zero8, B_z8), fin=0, hook=None),
            dict(src=ctx2[1], T=CTX, L=CTX, mi=1, seg=1, init=(zero8, B_z8), fin=1, hook=None),
        ]
        prev_fin = 3
        for s_ in range(7):
            f0 = (2 * s_) % 3
            f1 = (2 * s_ + 1) % 3
            if s_ == 0:
                hook = first_hook
            else:
                hook = (lambda s_=s_, pf=prev_fin: boundary(s_, pf, SI[s_ % 2], B_SI[s_ % 2]))
            all_subs.append(dict(src=x_out[s_, 0:512, :], T=512, L=64, mi=0, seg=2 + s_,
                                 init=(SI[s_ % 2], B_SI[s_ % 2]), fin=f0, hook=hook,
                                 pre=(lambda: wv_rescale(0)) if s_ == 0 else None))
            all_subs.append(dict(src=x_out[s_, 512:1024, :], T=512, L=64, mi=0, seg=2 + s_,
                                 init=(ST[f0], B_ST[f0]), fin=f1, hook=None))
            prev_fin = f1
        fence(COARSE(), FINE())
        run_pipeline(all_subs)
        fence(FINE(), COARSE())
        fence([B_gbc], [B_y])
        cr = SI[1]
        boundary(7, prev_fin, SI[1], B_SI[1])
        B_cr = B_SI[1]


        hlT = xnT
        B_hl = B_xh
        pb_prev = None
        pend_w = wload(wada_v, 16 * 256)
        for t in range(8):
            pb = t0_tile(x_own[t * 128:(t + 1) * 128, :], t * 128, hlT, B_xh, modulate=True, defer=True, pn="CD"[t % 2])
            if pb_prev is not None:
                pb_prev()
            pb_prev = pb
            cur_w = pend_w
            if t < 7:
                pend_w = wload(wada_v, (17 + t) * 256)
            mod_block(cur_w, 16 + t, pp="AB")
        pb_prev()
        k.do(DVE, lambda h: h.tensor_copy(out=gtv, in_=modv[:, 0, 32:48]), reads=[B_sm], writes=[B_cst])
        WB.append((xtw[0], B_xt[0], "wxt0"))
        T1, T2 = xt[1][:, 0:1024], xt[1][:, 1024:2048]
        B_T1, B_T2 = Buf("T1"), Buf("T2")
        fence([B_xt[1]], [B_T1, B_T2])
        load_bd(bdo, 0, NSEG, B_bdo)
        load_bd(bdo, 2, NSEG + 1, B_bdo)
        B_own = Buf("own")
        for d in range(2):
            seg_consts(NSEG + d, ownv[:, d], ownh[:, d], ownc[:, d], owne[:, d], B_own)
        k.dma(SP, w5o, seg_w5[NSEG], writes=[B_own], key="sc_own")

        def inproj(pn, cur, cc):
            mm_group(pn, TOK, lambda kc: WB[cur][0][:, kc, cc * 128:(cc + 1) * 128],
                     lambda kc, h0, hn: hlT[:, kc, h0:h0 + hn], [WB[cur][1], B_hlodd] + B_xh)

        def inproj1(pn, col0):
            cur = wload(win_v, col0, ncols=128)
            mm_group(pn, TOK, lambda kc: WB[cur][0][:, kc, 0:128],
                     lambda kc, h0, hn: hlT[:, kc, h0:h0 + hn], [WB[cur][1], B_hlodd] + B_xh)

        def gates_pe(j, slot, pa, pb):
            mm_group(pa, TOK, lambda kc: bdo[:, slot, j, :], lambda kc, h0, hn: xbb[:, h0:h0 + hn], [B_bdo, B_xbb] + bd_subs(B_bdo), nk=1)
            mm_group(pb, TOK, lambda kc: bdo[:, slot + 1, j, :], lambda kc, h0, hn: xbb[:, h0:h0 + hn], [B_bdo, B_xbb] + bd_subs(B_bdo), nk=1)

        def gates_act(j, d, pa, pb, ri, ii, ai):
            r, i_, a = tmp[ri], tmp[ii], tmp[ai]
            k.do(ACT, lambda h: h.activation(out=r[:], in_=PSM[pa][:], func=AF.Tanh, scale=0.5, bias=ownh[:, d, 0, j:j + 1]),
                 reads=[B_ps[pa], B_own], writes=[B_tmp[ri]])
            k.do(ACT, lambda h: h.activation(out=i_[:], in_=PSM[pb][:], func=AF.Tanh, scale=0.5, bias=ownh[:, d, 1, j:j + 1]),
                 reads=[B_ps[pb], B_own], writes=[B_tmp[ii]])
            k.do(ACT, lambda h: h.activation(out=a[:], in_=r[:], func=AF.Exp, scale=ownc[:, d, 0, j:j + 1],
                                             bias=ownc[:, d, 0, j:j + 1]), reads=[B_tmp[ri], B_own], writes=[B_tmp[ai]])
            k.do(ACT, lambda h: h.activation(out=r[:], in_=r[:], func=AF.Exp, scale=ownc[:, d, 1, j:j + 1],
                                             bias=ownc[:, d, 1, j:j + 1]), reads=[B_tmp[ri], B_own], writes=[B_tmp[ri]])

        def gates_sqrt(ri):
            r = tmp[ri]
            k.do(ACT, lambda h: h.activation(out=r[:], in_=r[:], func=AF.Sqrt, scale=-1.0, bias=ONE_AP),
                 reads=[B_tmp[ri], B_eps], writes=[B_tmp[ri]])

        def gates_dve(ri, ii, ai, init_ap, B_init, out_h, B_h, reverse):
            r, i_, a = tmp[ri], tmp[ii], tmp[ai]
            k.do(DVE, lambda h: h.scalar_tensor_tensor(out=i_[:], in0=i_[:], scalar=1.0, in1=tmp[1][:], op0=ALU.add,
                                                       op1=ALU.mult), reads=[B_tmp[ii], B_tmp[1]], writes=[B_tmp[ii]])
            k.do(DVE, lambda h: h.scalar_tensor_tensor(out=i_[:], in0=i_[:], scalar=0.5, in1=r[:], op0=ALU.mult,
                                                       op1=ALU.mult), reads=[B_tmp[ii], B_tmp[ri]], writes=[B_tmp[ii]])
            if not reverse:
                k.do(DVE, lambda h: h.tensor_tensor_scan(out=out_h[:], data0=a[:], data1=i_[:], initial=init_ap,
                                                         op0=ALU.mult, op1=ALU.add),
                     reads=[B_tmp[ai], B_tmp[ii], B_init], writes=[B_h])
            else:
                k.do(DVE, lambda h: h.tensor_tensor_scan(out=out_h[:, ::-1], data0=a[:, ::-1], data1=i_[:, ::-1],
                                                         initial=init_ap, op0=ALU.mult, op1=ALU.add),
                     reads=[B_tmp[ai], B_tmp[ii], B_init], writes=[B_h])

        v3 = lambda ap, a, b: ap.rearrange("p (r l) -> p r l", l=64)[:, :, a:b]

        def K1(jk):
            inproj1("B", 1024 + jk * 128)
            k.do(ACT, lambda h: h.activation(out=T1, in_=psB[:], func=AF.Copy), reads=[B_ps["B"]], writes=[B_T1])

        def K2(jk):
            inproj1("D", 2048 + jk * 128)
            k.do(DVE, lambda h: h.tensor_tensor(out=T1, in0=T1, in1=psD[:], op=ALU.mult),
                 reads=[B_ps["D"], B_T1], writes=[B_T1])
            k.do(DVE, lambda h: h.tensor_scalar(out=T2, in0=T1, scalar1=wcav[:, jk, 1:2], scalar2=None, op0=ALU.mult),
                 reads=[B_T1, B_cst], writes=[B_T2])
            k.do(DVE, lambda h: h.scalar_tensor_tensor(out=v3(T2, 1, 64), in0=v3(T1, 0, 63), scalar=wcav[:, jk, 0:1],
                                                       in1=v3(T2, 1, 64), op0=ALU.mult, op1=ALU.add),
                 reads=[B_T1, B_cst], writes=[B_T2])
            k.do(DVE, lambda h: h.scalar_tensor_tensor(out=v3(T2, 0, 63), in0=v3(T1, 1, 64), scalar=wcav[:, jk, 2:3],
                                                       in1=v3(T2, 0, 63), op0=ALU.mult, op1=ALU.add),
                 reads=[B_T1, B_cst], writes=[B_T2])

        def K3(jk):
            inproj1("B", 3072 + jk * 128)
            k.do(ACT, lambda h: h.activation(out=T1, in_=psB[:], func=AF.Silu), reads=[B_ps["B"]], writes=[B_T1])
            inproj1("D", 0 + jk * 128)

        def K4(jk):
            k.do(DVE, lambda h: h.tensor_tensor(out=T1, in0=T1, in1=psD[:], op=ALU.mult),
                 reads=[B_ps["D"], B_T1], writes=[B_T1])
            k.do(DVE, lambda h: h.tensor_tensor(out=yy[:, jk, :], in0=T1, in1=T2, op=ALU.mult),
                 reads=[B_T1, B_T2], writes=[B_y])

        for j in range(8):
            jk = j - 1
            inproj1("A", 4096 + j * 128)
            if jk >= 0:
                K1(jk)
            k.do(ACT, lambda h: h.activation(out=tmp[0][:], in_=psA[:], func=AF.Copy),
                 reads=[B_ps["A"]], writes=[B_tmp[0]])
            k.do(ACT, lambda h, j=j: h.activation(out=tmp[1][:], in_=psA[:], func=AF.Identity, scale=w5o[:, j, 2:3],
                                                  bias=bcv[:, j:j + 1]),
                 reads=[B_ps["A"], B_own, B_cst], writes=[B_tmp[1]])
            conv5(TOK, 64, j, w5o, B_own, tmp[0][:], tmp[1][:], B_tmp[0], B_tmp[1], skip_first=True)
            k.do(ACT, lambda h: h.activation(out=xbb[:, 0:1024], in_=tmp[1][:], func=AF.Copy),
                 reads=[B_tmp[1]], writes=[B_xbb])
            if jk >= 0:
                K2(jk)
            gates_pe(j, 0, "C", "D")
            gates_pe(j, 2, "B", "A")
            gates_act(j, 0, "C", "D", 2, 3, 4)
            inproj1("C", 5120 + j * 128)
            gates_act(j, 1, "B", "A", 5, 7, 0)
            gates_sqrt(2)
            gates_sqrt(5)
            if jk >= 0:
                K3(jk)
            gates_dve(2, 3, 4, cf[:, j:j + 1], B_cf, tmp[6], B_tmp[6], False)
            gates_dve(5, 7, 0, cr[:, j:j + 1], B_cr, tmp[8], B_tmp[8], True)
            k.do(DVE, lambda h: h.tensor_tensor(out=tmp[6][:], in0=tmp[6][:], in1=tmp[8][:], op=ALU.add),
                 reads=[B_tmp[8]], writes=[B_tmp[6]])
            if jk >= 0:
                K4(jk)
            k.do(ACT, lambda h: h.activation(out=tmp[2][:], in_=psC[:], func=AF.Silu),
                 reads=[B_ps["C"]], writes=[B_tmp[2]])
            k.do(DVE, lambda h, j=j: h.tensor_tensor(out=yy[:, 8 + j, :], in0=tmp[6][:], in1=tmp[2][:], op=ALU.mult),
                 reads=[B_tmp[6], B_tmp[2]], writes=[B_y])
        K1(7); K2(7); K3(7); K4(7)
        fence([B_T1, B_T2], [B_xt[1]])
        WB.append((xtw[1], B_xt[1], "wxt1"))

        fg, B_fg = tmp[8], B_tmp[8]
        k.dma(SP, fg[:], fgbc[:, 0:1024], writes=[B_fg], key="fg")
        fg2, B_fg2 = tmp[6], B_tmp[6]
        k.dma(SP, fg2[:], fgbc[:, 1024:2048], writes=[B_fg2], key="fg2")
        osbs, B_osbs = [tmp[0], tmp[1]], [B_tmp[0], B_tmp[1]]
        del WB[2:]
        for blk in range(8):
            cur = wload(wout_v, blk * 256)
            for cc in range(2):
                dch = blk * 2 + cc
                pn = "AB"[dch % 2]
                osb, B_osb = osbs[dch % 2], B_osbs[dch % 2]
                mm_group(pn, TOK, lambda kc, cur=cur, cc=cc: WB[cur][0][:, kc, cc * 128:(cc + 1) * 128],
                         lambda kc, h0, hn: yy[:, kc, h0:h0 + hn], [WB[cur][1], B_y])
                k.do(ACT, lambda h, pn=pn, dch=dch, osb=osb: h.activation(out=osb[:], in_=PSM[pn][:], func=AF.Identity,
                                                                          scale=gtv[:, dch:dch + 1]),
                     reads=[B_ps[pn], B_cst], writes=[B_osb])
                pt = "CD"[dch % 2]

                def emit(h, pt=pt, osb=osb):
                    for t in range(8):
                        ins = h.transpose(out=PSM[pt][:, t * 128:(t + 1) * 128], in_=osb[:, t * 128:(t + 1) * 128],
                                          identity=identf[:])
                    return ins
                pe_group(emit, [B_osb, B_id], [B_ps[pt]])
                k.do(DVE, lambda h, pt=pt, dch=dch: h.tensor_copy(
                    out=newlat[:, :, dch * 128:(dch + 1) * 128],
                    in_=PSM[pt][:, :].rearrange("p (t d) -> p t d", t=8)),
                    reads=[B_ps[pt]], writes=[B_wv, B_hlodd] + B_xh)
        for t in range(8):
            i = xcount[0] % 2
            xcount[0] += 1
            q = qcount[0] % NQ
            qcount[0] += 1
            k.dma(SP, xt[i][:], x_own[t * 128:(t + 1) * 128, :], writes=[B_xt[i]], key=f"xt{i}")
            k.do(DVE, lambda h, i=i, t=t: h.tensor_tensor(out=xt[i][:], in0=xt[i][:], in1=newlat[:, t, :], op=ALU.add),
                 reads=[B_wv] + B_xh, writes=[B_xt[i]])
            k.do(ACT, lambda h, i=i, q=q: h.activation(out=tmp[7][:, 0:1024].bitcast(BF16), in_=xt[i][:], func=AF.Square,
                                                       accum_out=ssq[q]),
                 reads=[B_xt[i]], writes=[B_tmp[7], B_q[q]])
            k.do(ACT, lambda h, q=q: h.activation(out=rstd[q], in_=ssq[q], func=AF.Sqrt, scale=1.0 / D, bias=EPS_AP),
                 reads=[B_q[q], B_eps], writes=[B_q[q]])
            k.do(DVE, lambda h, q=q: h.reciprocal(out=rstd[q], in_=rstd[q]), reads=[B_q[q]], writes=[B_q[q]])
            k.do(DVE, lambda h, i=i, q=q: h.scalar_tensor_tensor(out=xt[i][:, 0:1024], in0=xt[i][:, 0:1024], scalar=rstd[q],
                                                                 in1=fg[:], op0=ALU.mult, op1=ALU.mult),
                 reads=[B_q[q], B_fg], writes=[B_xt[i]])
            k.do(DVE, lambda h, i=i, q=q: h.scalar_tensor_tensor(out=xt[i][:, 1024:2048], in0=xt[i][:, 1024:2048],
                                                                 scalar=rstd[q], in1=fg2[:], op0=ALU.mult, op1=ALU.mult),
                 reads=[B_q[q], B_fg2], writes=[B_xt[i]])
            k.dma(SP, out_d[t * 128:(t + 1) * 128, :], xt[i][:], reads=[B_xt[i]], key=f"st{i}")
        for key in ("st0", "st1"):
            ds = k.dsems[key]
            SP.h.wait_ge(ds[0], ds[1])
    return nc


def kernel(x, c, ctx, c_ctx, norm_g, w_ada, b_ada, w_in, w_conv_a, w_conv_b, b_conv_b,
           lru_wa, lru_ba, lru_wx, lru_bx, lru_lambda, w_out, final_g):
    f = np.float32
    x = np.asarray(x, f)[0]
    ctx_ = np.asarray(ctx, f)[0]
    pm = lambda v, n: np.ascontiguousarray(np.asarray(v, f).reshape(n, 128).T)
    svec = np.ascontiguousarray(np.stack([pm(np.asarray(c)[0], 16), pm(c_ctx, 16)], axis=-1))
    gvec = pm(np.asarray(norm_g)[0], 16)
    bada = pm(np.asarray(b_ada)[0], 48)
    wcb = np.asarray(w_conv_b, f)[0]
    z = np.zeros_like(wcb[0])
    w5_nat = np.stack([wcb[0], wcb[1], wcb[2], wcb[3], z], 0)
    w5_rev = np.stack([z, wcb[3], wcb[2], wcb[1], wcb[0]], 0)
    lay5 = lambda w5: np.ascontiguousarray(w5.reshape(5, 8, 128).transpose(2, 1, 0))
    wca = np.ascontiguousarray(np.asarray(w_conv_a, f)[0].reshape(3, 8, 128).transpose(2, 1, 0))
    bconv = pm(np.asarray(b_conv_b)[0], 8)
    wa = np.asarray(lru_wa, f)[0]; wx = np.asarray(lru_wx, f)[0]
    ba = np.asarray(lru_ba, f)[0]; bx = np.asarray(lru_bx, f)[0]; lam = np.asarray(lru_lambda, f)[0]
    vec = lambda d: np.ascontiguousarray(np.stack([pm(ba[d], 8), pm(bx[d], 8), pm(lam[d], 8)], 1))
    fgbc = np.ascontiguousarray(np.broadcast_to(np.asarray(final_g, f)[None, :], (128, D)))
    eye = np.eye(128, dtype=f)
    ctx2 = np.ascontiguousarray(np.stack([ctx_, ctx_[::-1]], 0))
    xb = x.reshape(8, TOK, D)
    in_maps = []
    for kk in range(NCORES):
        dirs = [0, 1]
        revs = [False, True]
        blocks = []
        for s in range(7):
            if s < kk:
                dirs.append(0); revs.append(False); blocks.append(xb[s])
            else:
                dirs.append(1); revs.append(True); blocks.append(xb[7 - (s - kk)][::-1])
        dirs += [0, 1]; revs += [False, False]
        in_maps.append({
            "x_own": np.ascontiguousarray(xb[kk]),
            "x_out": np.ascontiguousarray(np.stack(blocks, 0)),
            "ctx2": ctx2, "svec": svec, "gvec": gvec,
            "w_ada": np.asarray(w_ada, f)[0], "b_ada": bada,
            "w_in": np.asarray(w_in, f)[0], "w_out": np.asarray(w_out, f)[0],
            "seg_wa": np.ascontiguousarray(np.stack([wa[d] for d in dirs], 0)),
            "seg_wx": np.ascontiguousarray(np.stack([wx[d] for d in dirs], 0)),
            "seg_vec": np.ascontiguousarray(np.stack([vec(d) for d in dirs], 0)),
            "seg_w5": np.ascontiguousarray(np.stack([lay5(w5_rev if r else w5_nat) for r in revs], 0)),
            "bconv": bconv, "wca": wca, "fgbc": fgbc,
            "onehot": np.ascontiguousarray(np.broadcast_to((np.arange(8) == kk).astype(f)[None, :], (128, 8))),
            "eye": eye,
        })
    nc = build_program()
    res = run_bass_kernel_spmd(nc, in_maps, core_ids=list(range(NCORES)))
    out = np.concatenate([res.results[r]["out"] for r in range(NCORES)], axis=0)
    return out.reshape(1, NCORES * TOK, D).astype(np.float32)
```

```python
from contextlib import ExitStack
import numpy as np
import concourse.bass as bass
import concourse.mybir as mybir
from concourse.bass_utils import run_bass_kernel_spmd

F32 = mybir.dt.float32
BF16 = mybir.dt.bfloat16
AF = mybir.ActivationFunctionType
ALU = mybir.AluOpType

NCORES = 8
D = 2048
TOK = 1024
CTX = 256
NSEG = 9
EPS = 1e-6


class Buf:
    def __init__(self, name):
        self.name = name
        self.w = None
        self.r = []


class Eng:
    def __init__(self, nc, es, handle, name):
        self.h = handle
        self.sem = es.enter_context(nc.semaphore("sem_" + name))
        self.n = 0
        self.seen = {}

    def wait(self, ev):
        sem, cnt = ev
        key = id(sem)
        if sem is self.sem:
            pass
        if self.seen.get(key, 0) >= cnt:
            return
        self.seen[key] = cnt
        self.h.wait_ge(sem, cnt)


class K:
    def __init__(self, nc, es):
        self.nc = nc
        self.es = es
        self.pe = Eng(nc, es, nc.tensor, "pe")
        self.act = Eng(nc, es, nc.scalar, "act")
        self.dve = Eng(nc, es, nc.vector, "dve")
        self.pool = Eng(nc, es, nc.gpsimd, "pool")
        self.sp = Eng(nc, es, nc.sync, "sp")
        self.dsems = {}

    def _deps(self, eng, reads, writes):
        evs = []
        for b in reads:
            if b.w is not None:
                evs.append(b.w)
        for b in writes:
            if b.w is not None:
                evs.append(b.w)
            evs.extend(b.r)
        for ev in evs:
            eng.wait(ev)

    def op(self, eng, fn, reads=(), writes=(), signal=True, same_pe_ok=False):
        self._deps(eng, reads, writes)
        ins = fn(eng.h)
        if signal:
            eng.n += 1
            ins.then_inc(eng.sem, 1)
            ev = (eng.sem, eng.n)
            eng.seen[id(eng.sem)] = max(eng.seen.get(id(eng.sem), 0), 0)
        else:
            ev = None
        return ev

    def commit(self, ev, reads=(), writes=()):
        for b in writes:
            b.w = ev
            b.r = []
        for b in reads:
            b.r.append(ev)

    def do(self, eng, fn, reads=(), writes=()):
        ev = self.op(eng, fn, reads, writes)
        self.commit(ev, reads, writes)
        return ev

    def dma(self, eng, out, in_, reads=(), writes=(), key=None, **kw):
        key = key or (writes[0].name if writes else reads[0].name)
        if key not in self.dsems:
            self.dsems[key] = [self.es.enter_context(self.nc.semaphore("d_" + key)), 0]
        ds = self.dsems[key]
        self._deps(eng, reads, writes)
        ds[1] += 16
        eng.h.dma_start(out=out, in_=in_, **kw).then_inc(ds[0], 16)
        ev = (ds[0], ds[1])
        self.commit(ev, reads, writes)
        return ev


def build_program():
    nc = bass.Bass("TRN2", target_bir_lowering=False)
    di = lambda n, s: nc.dram_tensor(n, s, F32, kind="ExternalInput").ap()
    x_own = di("x_own", [TOK, D])
    x_out = di("x_out", [7, TOK, D])
    ctx2 = di("ctx2", [2, CTX, D])
    svec = di("svec", [128, 16, 2])
    gvec = di("gvec", [128, 16])
    w_ada = di("w_ada", [D, 3 * D])
    b_ada = di("b_ada", [128, 48])
    w_in = di("w_in", [D, 6144])
    w_out = di("w_out", [D, D])
    seg_wa = di("seg_wa", [NSEG + 2, 16, 64, 64])
    seg_wx = di("seg_wx", [NSEG + 2, 16, 64, 64])
    seg_vec = di("seg_vec", [NSEG + 2, 128, 3, 8])
    seg_w5 = di("seg_w5", [NSEG + 2, 128, 8, 5])
    bconv = di("bconv", [128, 8])
    wca = di("wca", [128, 8, 3])
    fgbc = di("fgbc", [128, D])
    onehot = di("onehot", [128, 8])
    eye = di("eye", [128, 128])
    out_d = nc.dram_tensor("out", [TOK, D], F32, kind="ExternalOutput").ap()

    with ExitStack() as es:
        k = K(nc, es)
        PE, ACT, DVE, POOL, SP = k.pe, k.act, k.dve, k.pool, k.sp
        sb = lambda n, s, dt=F32: es.enter_context(nc.sbuf_tensor(n, s, dt))
        pst = lambda n, s, dt=F32: es.enter_context(nc.psum_tensor(n, s, dt))

        big = sb("big", [128, 16384], F32)
        wv = big[:, 0:8192].bitcast(BF16).rearrange("p (k c) -> p k c", k=16)
        xnT = big[:, 8192:16384].bitcast(BF16).rearrange("p (k t) -> p k t", k=16)
        xnTh = [big[:, 8192 + 4096 * i:8192 + 4096 * (i + 1)].bitcast(BF16).rearrange("p (k t) -> p k t", k=16)
                for i in range(2)]
        newlat = big[:, :].rearrange("p (t d) -> p t d", t=8)
        B_wv = Buf("wv")
        B_xh = [Buf("xnT0"), Buf("xnT1")]
        yy = sb("yy", [128, 16, 1024], BF16)
        B_y = Buf("yy")
        yflat = yy[:].rearrange("p k t -> p (k t)")
        stg = yflat.bitcast(F32)[:, 0:4096].rearrange("p (k c) -> p k c", k=16)
        raw = yflat[:, 8192:12288].rearrange("p (k c) -> p k c", k=16)
        xt = [sb(f"xt{i}", [128, D]) for i in range(2)]
        B_xt = [Buf(f"xt{i}") for i in range(2)]
        xnbs = [sb(f"xnb{i}", [128, D], BF16) for i in range(2)]
        B_xnbs = [Buf(f"xnb{i}") for i in range(2)]
        wst = [sb(f"wst{i}", [128, 16, 256], BF16) for i in range(2)]
        B_wst = [Buf(f"wst{i}") for i in range(2)]
        NT = 9
        tmp = [sb(f"tmp{i}", [128, 1024]) for i in range(NT)]
        B_tmp = [Buf(f"tmp{i}") for i in range(NT)]
        xbb = sb("xbb", [128, 2048], BF16)
        B_xbb = Buf("xbb")
        NBD = 3
        bd = [sb(f"bd{i}", [128, 2, 8, 128], BF16) for i in range(NBD)]
        B_bd = [Buf(f"bd{i}") for i in range(NBD)]
        bdo = sb("bdo", [128, 4, 8, 128], BF16)
        B_bdo = Buf("bdo")
        identb = sb("identb", [128, 128], BF16)
        identf = sb("identf", [128, 128])
        B_id = Buf("ident")
        small = sb("small", [128, 1024])
        B_sm = Buf("small")
        sv = small[:, 0:32].rearrange("p (k v) -> p k v", v=2)
        svb = sb("svb", [128, 16, 2], BF16)
        gv = small[:, 32:48]
        modv = small[:, 48:144].rearrange("p (v c) -> p v c", v=2)
        gam = small[:, 144:176].rearrange("p (v k) -> p v k", v=2)
        shb = sb("shb", [128, 16, 2], BF16)
        bv = small[:, 176:192].rearrange("p (j v) -> p j v", v=2)
        B_bv = Buf("bv")
        bcv = small[:, 192:200]
        wcav = small[:, 200:224].rearrange("p (j t) -> p j t", t=3)
        ohv = small[:, 224:232]
        B_cst = Buf("cst")
        gtv = small[:, 232:248]
        ownv = small[:, 256:304].rearrange("p (d a j) -> p d a j", d=2, a=3)
        ownc = small[:, 304:336].rearrange("p (d a j) -> p d a j", d=2, a=2)
        owne = small[:, 336:352].rearrange("p (d j) -> p d j", d=2)
        ownh = small[:, 352:384].rearrange("p (d a j) -> p d a j", d=2, a=2)
        w5o = small[:, 384:424].rearrange("p (j t) -> p j t", t=5)
        NR = 3
        segv = [small[:, 448 + 24 * i:448 + 24 * (i + 1)].rearrange("p (a j) -> p a j", a=3) for i in range(NR)]
        segh = [small[:, 520 + 16 * i:520 + 16 * (i + 1)].rearrange("p (a j) -> p a j", a=2) for i in range(NR)]
        segc = [small[:, 568 + 16 * i:568 + 16 * (i + 1)].rearrange("p (a j) -> p a j", a=2) for i in range(NR)]
        sege = [small[:, 616 + 8 * i:616 + 8 * (i + 1)] for i in range(NR)]
        w5r = [small[:, 640 + 40 * i:640 + 40 * (i + 1)].rearrange("p (j t) -> p j t", t=5) for i in range(NR)]
        B_seg = [Buf(f"seg{i}") for i in range(NR)]
        segcb = [small[:, 896 + 8 * i:896 + 8 * (i + 1)] for i in range(NR)]
        NST = 4
        ST = [small[:, 768 + 8 * i:768 + 8 * (i + 1)] for i in range(NST)]
        B_ST = [Buf(f"ST{i}") for i in range(NST)]
        SI = [small[:, 800 + 8 * i:800 + 8 * (i + 1)] for i in range(2)]
        B_SI = [Buf(f"SI{i}") for i in range(2)]
        cf = small[:, 816:824]; B_cf = Buf("cf")
        ctxr = small[:, 824:832]; B_ctxr = Buf("ctxr")
        Dt = small[:, 832:840]; B_Dt = Buf("Dt")
        zero8 = small[:, 840:848]; B_z8 = Buf("z8")
        NQ = 4
        ssq = [small[:, 856 + i:857 + i] for i in range(NQ)]
        rstd = [small[:, 864 + i:865 + i] for i in range(NQ)]
        B_q = [Buf(f"q{i}") for i in range(NQ)]
        cst = small[:, 880:884]
        B_eps = Buf("eps")
        EPS_AP, ONE_AP, NHALF_AP = cst[:, 0:1], cst[:, 1:2], cst[:, 2:3]

        psA = pst("psA", [128, 1024])
        psB = pst("psB", [128, 1024])
        psC = pst("psC", [128, 1024])
        psD = pst("psD", [128, 1024])
        B_ps = {n: Buf(n) for n in "ABCD"}
        PSM = {"A": psA, "B": psB, "C": psC, "D": psD}
        psV = [psA[:, 0:512], psA[:, 512:1024]]
        psGa = [psB[:, 0:512], psB[:, 512:1024]]
        psGx = [psC[:, 0:512], psC[:, 512:1024]]
        B_pV = [Buf("pV0"), Buf("pV1")]
        B_pGa = [Buf("pGa0"), Buf("pGa1")]
        B_pGx = [Buf("pGx0"), Buf("pGx1")]

        k.do(POOL, lambda h: h.memset(cst[:, 0:1], EPS), writes=[B_eps])
        k.do(POOL, lambda h: h.memset(cst[:, 1:2], 1.0), writes=[B_eps])
        k.do(POOL, lambda h: h.memset(cst[:, 2:3], -0.5), writes=[B_eps])
        k.do(POOL, lambda h: h.memset(zero8, 0.0), writes=[B_z8])
        k.dma(SP, identf[:], eye, writes=[B_id], key="idf")
        k.dma(POOL, identb[:], eye, writes=[B_id], key="c1")
        k.dma(SP, sv, svec, writes=[B_sm], key="c0")
        k.dma(SP, gv, gvec, writes=[B_sm], key="c0")
        k.dma(SP, bcv, bconv, writes=[B_cst], key="c2")
        k.dma(SP, wcav, wca, writes=[B_cst], key="c2")
        k.dma(SP, ohv, onehot, writes=[B_cst], key="c2")
        k.dma(SP, modv[:, 0, :], b_ada, writes=[B_sm], key="c0")
        k.dma(SP, modv[:, 1, :], b_ada, writes=[B_sm], key="c0")
        for i in range(NBD):
            k.do(POOL, lambda h, i=i: h.memset(bd[i][:], 0.0), writes=[B_bd[i]])
        k.do(POOL, lambda h: h.memset(bdo[:], 0.0), writes=[B_bdo])
        k.do(ACT, lambda h: h.activation(out=tmp[0][:, 0:32], in_=small[:, 0:32], func=AF.Sigmoid),
             reads=[B_sm], writes=[B_tmp[0]])
        k.do(DVE, lambda h: h.tensor_tensor(out=small[:, 0:32], in0=small[:, 0:32], in1=tmp[0][:, 0:32],
                                            op=ALU.mult), reads=[B_tmp[0]], writes=[B_sm])
        B_svb = Buf("svb")
        k.do(DVE, lambda h: h.tensor_copy(out=svb[:], in_=sv), reads=[B_sm], writes=[B_svb])

        win_v = w_in.rearrange("(kc p) c -> p kc c", p=128)
        wout_v = w_out.rearrange("(kc p) c -> p kc c", p=128)
        wada_v = w_ada.rearrange("(kc p) c -> p kc c", p=128)
        wcount = [0]

        xtw = [xt[i][:].bitcast(BF16).rearrange("p (k c) -> p k c", k=16) for i in range(2)]
        yyA = yflat[:, 4096:8192].rearrange("p (k c) -> p k c", k=16)
        yyB = yflat[:, 8192:12288].rearrange("p (k c) -> p k c", k=16)
        B_yA, B_yB = Buf("yyA"), Buf("yyB")
        WB = [(wst[0], B_wst[0], "wst0"), (wst[1], B_wst[1], "wst1"), (yyA, B_yA, "wyA"), (yyB, B_yB, "wyB")]

        def wload(src_view, c0, ncols=256):
            i = wcount[0] % len(WB)
            wcount[0] += 1
            k.dma(POOL, WB[i][0][:, :, 0:ncols], src_view[:, :, c0:c0 + ncols], writes=[WB[i][1]], key=WB[i][2])
            return i

        def pe_group(emit, reads, writes):
            k._deps(PE, reads, writes)
            ins = emit(PE.h)
            PE.n += 1
            ins.then_inc(PE.sem, 1)
            k.commit((PE.sem, PE.n), reads, writes)

        def mm_group(pn, ncols, lhs_fn, rhs_fn, reads, nk=16, out_ap=None, B_out=None):
            halves = [(h0, min(512, ncols - h0)) for h0 in range(0, ncols, 512)]
            dst = PSM[pn] if out_ap is None else out_ap
            Bo = B_ps[pn] if B_out is None else B_out

            def emit(h):
                ins = None
                for kc in range(nk):
                    for (h0, hn) in halves:
                        ins = h.matmul(dst[:, h0:h0 + hn], lhsT=lhs_fn(kc), rhs=rhs_fn(kc, h0, hn),
                                       start=(kc == 0), stop=(kc == nk - 1))
                return ins
            pe_group(emit, reads, [Bo])

        def mod_cols(col_list, pp="CD"):
            blocks = sorted(set(c // 2 for c in col_list))
            depth = max(1, len(WB) - 1)
            pend = []
            nxt_ = 0
            for bi, blk in enumerate(blocks):
                while nxt_ < len(blocks) and len(pend) < depth:
                    pend.append(wload(wada_v, blocks[nxt_] * 256))
                    nxt_ += 1
                cur = pend.pop(0)
                for cc in range(2):
                    col = blk * 2 + cc
                    pn = pp[col % 2]
                    mm_group(pn, 2, lambda kc, cc=cc, cur=cur: WB[cur][0][:, kc, cc * 128:(cc + 1) * 128],
                             lambda kc, h0, hn: svb[:, kc, :], [WB[cur][1], B_svb])
                    k.do(DVE, lambda h, col=col, pn=pn: h.tensor_tensor(
                        out=modv[:, :, col], in0=PSM[pn][:, 0:2], in1=modv[:, :, col], op=ALU.add),
                        reads=[B_ps[pn]], writes=[B_sm])

        B_wvp = [Buf(f"wvp{i}") for i in range(4)]

        def load_wv_raw():
            for piece in range(4):
                c0 = 4096 + piece * 256
                k.dma(POOL, wv[:, :, piece * 256:(piece + 1) * 256], win_v[:, :, c0:c0 + 256],
                      reads=[], writes=[B_wvp[piece]], key=f"wvp{piece}")

        def mod_block(cur, blk, pp="CD"):
            for cc in range(2):
                col = blk * 2 + cc
                pn = pp[col % 2]
                mm_group(pn, 2, lambda kc, cc=cc, cur=cur: WB[cur][0][:, kc, cc * 128:(cc + 1) * 128],
                         lambda kc, h0, hn: svb[:, kc, :], [WB[cur][1], B_svb])
                k.do(DVE, lambda h, col=col, pn=pn: h.tensor_tensor(
                    out=modv[:, :, col], in0=PSM[pn][:, 0:2], in1=modv[:, :, col], op=ALU.add),
                    reads=[B_ps[pn]], writes=[B_sm])

        mod_cols(range(16, 32))
        load_wv_raw()
        for v in range(2):
            k.do(DVE, lambda h, v=v: h.scalar_tensor_tensor(
                out=gam[:, v, :], in0=modv[:, v, 16:32], scalar=1.0, in1=gv, op0=ALU.add, op1=ALU.mult),
                reads=[B_sm], writes=[B_cst])
        B_shb = Buf("shb")

        def late_mod():
            mod_cols(range(0, 16))
            for v in range(2):
                k.do(DVE, lambda h, v=v: h.tensor_copy(out=shb[:, :, v], in_=modv[:, v, 0:16]), reads=[B_sm],
                     writes=[B_shb])
            wv_bias()
            del WB[2:]
            fence([B_yA, B_yB], [B_y])

        stg2 = [yflat.bitcast(F32)[:, 4096 * i:4096 * (i + 1)].rearrange("p (k c) -> p k c", k=16) for i in range(2)]
        B_stg = [Buf("stg0"), Buf("stg1")]
        sgc = [0]

        def wv_bias():
            for j in range(8):
                pn = "CD"[j % 2]
                mm_group(pn, 2, lambda kc, j=j: wv[:, kc, j * 128:(j + 1) * 128], lambda kc, h0, hn: shb[:, kc, :],
                         [B_wv, B_shb] + B_wvp)
                k.do(DVE, lambda h, j=j, pn=pn: h.tensor_copy(out=bv[:, j, :], in_=PSM[pn][:, 0:2]),
                     reads=[B_ps[pn]], writes=[B_bv])

        def wv_rescale(v):
            for piece in range(4):
                wp = wv[:, :, piece * 256:(piece + 1) * 256]
                k.do(DVE, lambda h, wp=wp, v=v: h.tensor_tensor(
                    out=wp, in0=wp, in1=gam[:, v, :].unsqueeze(2).broadcast_to([128, 16, 256]), op=ALU.mult),
                    reads=[B_wv, B_cst] + B_wvp, writes=[B_wv])

        gbc = yflat.bitcast(F32)[:, 0:2048]
        B_gbc = Buf("gbc")

        def build_gbc():
            ones_f = tmp[7][:, 0:128]
            k.do(POOL, lambda h: h.memset(ones_f, 1.0), writes=[B_tmp[7]])
            for c in range(16):
                dg = tmp[7][:, 128 + 128 * (c % 2):256 + 128 * (c % 2)]
                pn = "CD"[c % 2]
                k.do(DVE, lambda h, dg=dg, c=c: h.tensor_scalar(out=dg, in0=identf[:], scalar1=gam[:, 1, c:c + 1],
                                                                scalar2=None, op0=ALU.mult),
                     reads=[B_id, B_cst], writes=[B_tmp[7]])
                pe_group(lambda h, dg=dg, pn=pn: h.matmul(PSM[pn][:, 0:128], lhsT=ones_f, rhs=dg, start=True, stop=True),
                         [B_tmp[7]], [B_ps[pn]])
                k.do(DVE, lambda h, pn=pn, c=c: h.tensor_copy(out=gbc[:, c * 128:(c + 1) * 128], in_=PSM[pn][:, 0:128]),
                     reads=[B_ps[pn]], writes=[B_gbc])

        B_hlodd = Buf("hlodd")
        xcount = [0]
        qcount = [0]
        tcount = [0]

        def t0_tile(src, T_off, dst, B_dst, modulate, defer=False, pn="D", mi=0):
            Bd_ = B_dst if isinstance(B_dst, list) else [B_dst]
            i = xcount[0] % 2
            xcount[0] += 1
            q = qcount[0] % NQ
            qcount[0] += 1
            xnb, B_xnb = xnbs[i], B_xnbs[i]
            k.dma(SP, xt[i][:], src, writes=[B_xt[i]], key=f"xt{i}")
            k.do(ACT, lambda h: h.activation(out=tmp[7][:, 0:1024].bitcast(BF16), in_=xt[i][:], func=AF.Square,
                                             accum_out=ssq[q]),
                 reads=[B_xt[i]], writes=[B_tmp[7], B_q[q]])
            if modulate:
                k.do(ACT, lambda h: h.activation(out=rstd[q], in_=ssq[q], func=AF.Sqrt, scale=1.0 / D, bias=EPS_AP),
                     reads=[B_q[q], B_eps], writes=[B_q[q]])
                k.do(DVE, lambda h: h.reciprocal(out=rstd[q], in_=rstd[q]), reads=[B_q[q]], writes=[B_q[q]])
            else:
                k.do(POOL, lambda h: h.tensor_scalar(out=rstd[q], in0=ssq[q], scalar1=1.0 / D, scalar2=EPS, op0=ALU.mult,
                                                     op1=ALU.add), reads=[B_q[q]], writes=[B_q[q]])
                k.do(POOL, lambda h: h.tensor_tensor(out=rstd[q], in0=rstd[q], in1=NHALF_AP, op=ALU.pow),
                     reads=[B_q[q], B_eps], writes=[B_q[q]])
            if mi == 1:
                k.do(DVE, lambda h: h.scalar_tensor_tensor(out=xnb[:], in0=xt[i][:], scalar=rstd[q], in1=gbc,
                                                           op0=ALU.mult, op1=ALU.mult),
                     reads=[B_xt[i], B_q[q], B_gbc], writes=[B_xnb])
            else:
                k.do(DVE, lambda h: h.tensor_scalar(out=xnb[:], in0=xt[i][:], scalar1=rstd[q], scalar2=None,
                                                    op0=ALU.mult), reads=[B_xt[i], B_q[q]], writes=[B_xnb])
            pv = PSM[pn][:, :].bitcast(BF16)[:, 0:2048].rearrange("p (k t) -> p k t", k=16)

            def emit(h):
                for kc in range(16):
                    ins = h.transpose(out=pv[:, kc, :], in_=xnb[:, kc * 128:(kc + 1) * 128], identity=identb[:])
                return ins
            pe_group(emit, [B_xnb, B_id], [B_ps[pn]])

            def phase_b():
                if not modulate:
                    tcount[0] += 1
                    if True:
                        k.do(DVE, lambda h: h.tensor_copy(out=dst[:, :, T_off:T_off + 128], in_=pv),
                             reads=[B_ps[pn]], writes=Bd_)
                    else:
                        k.do(ACT, lambda h: h.activation(out=dst[:, :, T_off:T_off + 128], in_=pv, func=AF.Copy),
                             reads=[B_ps[pn]], writes=Bd_)
                else:
                    k._deps(ACT, [B_ps[pn], B_cst, B_sm], Bd_)
                    k._deps(DVE, [B_ps[pn], B_cst, B_sm], [B_hlodd])
                    for kc in range(0, 8):
                        ins = ACT.h.activation(out=dst[:, kc, T_off:T_off + 128], in_=pv[:, kc, :], func=AF.Identity,
                                               scale=gam[:, 0, kc:kc + 1], bias=modv[:, 0, kc:kc + 1])
                    ACT.n += 1
                    ins.then_inc(ACT.sem, 1)
                    for kc in range(8, 16):
                        ins2 = DVE.h.tensor_scalar(out=dst[:, kc, T_off:T_off + 128], in0=pv[:, kc, :],
                                                   scalar1=gam[:, 0, kc:kc + 1], scalar2=modv[:, 0, kc:kc + 1],
                                                   op0=ALU.mult, op1=ALU.add)
                    DVE.n += 1
                    ins2.then_inc(DVE.sem, 1)
                    k.commit((ACT.sem, ACT.n), [B_ps[pn], B_cst, B_sm], Bd_)
                    k.commit((DVE.sem, DVE.n), [B_ps[pn], B_cst, B_sm], [B_hlodd])
            if defer:
                return phase_b
            phase_b()
            return None

        def seg_consts(si, dst_v, dst_h, dst_c, dst_e, Bd, phase=0):
            if phase in (0, 1):
                k.dma(SP, dst_v, seg_vec[si], writes=[Bd], key="sc_" + Bd.name)
            if phase == 1:
                return
            k.do(ACT, lambda h: h.activation(out=dst_e, in_=dst_v[:, 2, :], func=AF.Exp, scale=-1.0),
                 reads=[Bd], writes=[Bd])
            k.do(ACT, lambda h: h.activation(out=dst_e, in_=dst_e, func=AF.Ln, bias=ONE_AP),
                 reads=[Bd, B_eps], writes=[Bd])
            k.do(DVE, lambda h: h.tensor_scalar(out=dst_c[:, 0, :], in0=dst_e, scalar1=-4.0, scalar2=None, op0=ALU.mult),
                 reads=[Bd], writes=[Bd])
            k.do(DVE, lambda h: h.tensor_scalar(out=dst_c[:, 1, :], in0=dst_e, scalar1=-8.0, scalar2=None, op0=ALU.mult),
                 reads=[Bd], writes=[Bd])
            k.do(DVE, lambda h: h.tensor_scalar(out=dst_h, in0=dst_v[:, 0:2, :], scalar1=0.5, scalar2=None, op0=ALU.mult),
                 reads=[Bd], writes=[Bd])

        BDS = {}

        def bd_subs(B_dst):
            if B_dst.name not in BDS:
                BDS[B_dst.name] = [Buf(f"{B_dst.name}_s{i}") for i in range(8)]
            return BDS[B_dst.name]

        def load_bd(dst4, slot0, si, B_dst):
            subs_ = bd_subs(B_dst)
            k._deps(POOL, [], [B_dst])
            B_dst.r = []
            for m, src in enumerate((seg_wa, seg_wx)):
                s4 = src[si].rearrange("(j two) i o -> two i j o", two=2)
                for two in range(2):
                    sbuf_ = subs_[(slot0 // 2) * 4 + m * 2 + two]
                    k.dma(POOL, dst4[two * 64:(two + 1) * 64, slot0 + m, :, two * 64:(two + 1) * 64], s4[two],
                          writes=[sbuf_], key="bd_" + sbuf_.name)

        def conv5(T, L, j, w5ap, Bw, vsb, xb, Bv, Bx, skip_first=False):
            v3 = lambda ap, a, b: ap.rearrange("p (r l) -> p r l", l=L)[:, :, a:b]
            if not skip_first:
              k.do(DVE, lambda h: h.tensor_scalar(out=xb, in0=vsb, scalar1=w5ap[:, j, 2:3],
                                                scalar2=bcv[:, j:j + 1], op0=ALU.mult, op1=ALU.add),
                 reads=[Bv, Bw, B_cst], writes=[Bx])
            for tap, off in ((1, -1), (0, -2), (3, 1), (4, 2)):
                if off < 0:
                    o_ap, i_ap = v3(xb, -off, L), v3(vsb, 0, L + off)
                else:
                    o_ap, i_ap = v3(xb, 0, L - off), v3(vsb, off, L)
                k.do(DVE, lambda h, o_ap=o_ap, i_ap=i_ap, tap=tap: h.scalar_tensor_tensor(
                    out=o_ap, in0=i_ap, scalar=w5ap[:, j, tap:tap + 1], in1=o_ap, op0=ALU.mult, op1=ALU.add),
                    reads=[Bv, Bw], writes=[Bx])

        half = lambda t, i: t[:, i * 512:(i + 1) * 512]
        V_ = [half(tmp[0], 0), half(tmp[0], 1)]
        H_ = [half(tmp[1], 0), half(tmp[1], 1)]
        XB_ = [half(tmp[2], 0), half(tmp[2], 1), half(tmp[3], 0), half(tmp[3], 1)]
        TI_ = [half(tmp[4], 0), half(tmp[4], 1), half(tmp[5], 0), half(tmp[5], 1)]
        A_ = [half(tmp[6], 0), half(tmp[6], 1), half(tmp[8], 0), half(tmp[8], 1)]
        ytail = yflat.bitcast(F32)[:, 6144:8192]
        TR_ = [ytail[:, i * 512:(i + 1) * 512] for i in range(4)]
        XBB_ = [xbb[:, 512 * i:512 * (i + 1)] for i in range(4)]
        mk = lambda n, c: [Buf(f"{n}{i}") for i in range(c)]
        B_V, B_H, B_XB, B_TI, B_A, B_TR, B_XBB = mk("V", 2), mk("H", 2), mk("XB", 4), mk("TI", 4), mk("A", 4), mk("TR", 4), mk("XBB", 4)
        ALLP = B_V + B_H + B_XB + B_TI + B_A + B_TR + B_XBB

        def run_pipeline(subs, pre_t0=0, mid=None):
            units = []
            for ssi, ss in enumerate(subs):
                for j in range(8):
                    units.append((ssi, j))
            NU = len(units)

            def t0_sub(ssi):
                ss = subs[ssi]
                for t in range(ss["T"] // 128):
                    yield (ssi, t)
            t0_list = []
            for ssi in range(len(subs)):
                t0_list.extend(list(t0_sub(ssi)))
            t0_pos = [0]
            pending_b = []

            def flush_t0():
                while pending_b:
                    pending_b.pop(0)()


            def emit_t0(n):
                for _ in range(n):
                    if t0_pos[0] >= len(t0_list):
                        return
                    ssi, t = t0_list[t0_pos[0]]
                    t0_pos[0] += 1
                    ss = subs[ssi]
                    flush_t0()
                    pb = t0_tile(ss["src"][t * 128:(t + 1) * 128, :], t * 128, xnTh[ssi % 2], B_xh[ssi % 2],
                                 modulate=False, defer=True, mi=ss["mi"])
                    pending_b.append(pb)

            def t0_done_for(ssi):
                return sum(subs[i]["T"] // 128 for i in range(ssi + 1))

            def P1(u):
                ssi, j = units[u]
                ss = subs[ssi]
                T = ss["T"]
                need = t0_done_for(ssi)
                if t0_pos[0] < need:
                    emit_t0(need - t0_pos[0])
                if j == 0:
                    flush_t0()
                    if ss.get("pre") is not None:
                        ss["pre"]()
                mm_group(None, T, lambda kc: wv[:, kc, j * 128:(j + 1) * 128],
                         lambda kc, h0, hn: xnTh[ssi % 2][:, kc, h0:h0 + hn], [B_wv, B_xh[ssi % 2]] + B_wvp,
                         out_ap=psV[u % 2], B_out=B_pV[u % 2])

            def S1(u):
                ssi, j = units[u]
                ss = subs[ssi]
                T, L, mi, r = ss["T"], ss["L"], ss["mi"], ss["seg"] % NR
                p2, p4 = u % 2, u % 4
                k.do(ACT, lambda h: h.activation(out=V_[p2][:, 0:T], in_=psV[p2][:, 0:T], func=AF.Identity,
                                                 bias=bv[:, j, mi:mi + 1]),
                     reads=[B_pV[p2], B_bv], writes=[B_V[p2]])
                k.do(ACT, lambda h: h.activation(out=XB_[p4][:, 0:T], in_=psV[p2][:, 0:T], func=AF.Identity,
                                                 scale=w5r[r][:, j, 2:3], bias=segcb[r][:, j:j + 1]),
                     reads=[B_pV[p2], B_seg[r]], writes=[B_XB[p4]])
                conv5(T, L, j, w5r[r], B_seg[r], V_[p2][:, 0:T], XB_[p4][:, 0:T], B_V[p2], B_XB[p4], skip_first=True)
                k.do(POOL, lambda h: h.tensor_copy(out=XBB_[p4][:, 0:T], in_=XB_[p4][:, 0:T]),
                     reads=[B_XB[p4]], writes=[B_XBB[p4]])

            def S2pe(u):
                ssi, j = units[u]
                ss = subs[ssi]
                T, b = ss["T"], ss["seg"] % NBD
                p2, p4 = u % 2, u % 4
                mm_group(None, T, lambda kc: bd[b][:, 0, j, :], lambda kc, h0, hn: XBB_[p4][:, h0:h0 + hn],
                         [B_bd[b], B_XBB[p4]] + bd_subs(B_bd[b]), nk=1, out_ap=psGa[p2], B_out=B_pGa[p2])
                mm_group(None, T, lambda kc: bd[b][:, 1, j, :], lambda kc, h0, hn: XBB_[p4][:, h0:h0 + hn],
                         [B_bd[b], B_XBB[p4]] + bd_subs(B_bd[b]), nk=1, out_ap=psGx[p2], B_out=B_pGx[p2])

            def S2(u):
                ssi, j = units[u]
                ss = subs[ssi]
                T, r = ss["T"], ss["seg"] % NR
                p2, p4 = u % 2, u % 4
                k.do(ACT, lambda h: h.activation(out=TR_[p4][:, 0:T], in_=psGa[p2][:, 0:T], func=AF.Tanh, scale=0.5,
                                                 bias=segh[r][:, 0, j:j + 1]),
                     reads=[B_pGa[p2], B_seg[r]], writes=[B_TR[p4]])
                k.do(ACT, lambda h: h.activation(out=TI_[p4][:, 0:T], in_=psGx[p2][:, 0:T], func=AF.Tanh, scale=0.5,
                                                 bias=segh[r][:, 1, j:j + 1]),
                     reads=[B_pGx[p2], B_seg[r]], writes=[B_TI[p4]])
                k.do(ACT, lambda h: h.activation(out=A_[p4][:, 0:T], in_=TR_[p4][:, 0:T], func=AF.Exp,
                                                 scale=segc[r][:, 0, j:j + 1], bias=segc[r][:, 0, j:j + 1]),
                     reads=[B_TR[p4], B_seg[r]], writes=[B_A[p4]])
                k.do(ACT, lambda h: h.activation(out=TR_[p4][:, 0:T], in_=TR_[p4][:, 0:T], func=AF.Exp,
                                                 scale=segc[r][:, 1, j:j + 1], bias=segc[r][:, 1, j:j + 1]),
                     reads=[B_TR[p4], B_seg[r]], writes=[B_TR[p4]])
                k.do(DVE, lambda h: h.scalar_tensor_tensor(out=TI_[p4][:, 0:T], in0=TI_[p4][:, 0:T], scalar=1.0,
                                                           in1=XB_[p4][:, 0:T], op0=ALU.add, op1=ALU.mult),
                     reads=[B_TI[p4], B_XB[p4]], writes=[B_TI[p4]])

            def SQ(u):
                ssi, j = units[u]
                T = subs[ssi]["T"]
                p4 = u % 4
                k.do(ACT, lambda h: h.activation(out=TR_[p4][:, 0:T], in_=TR_[p4][:, 0:T], func=AF.Sqrt, scale=-1.0,
                                                 bias=ONE_AP), reads=[B_TR[p4], B_eps], writes=[B_TR[p4]])

            def D2(u):
                ssi, j = units[u]
                ss = subs[ssi]
                T = ss["T"]
                p2, p4 = u % 2, u % 4
                if j == 0 and ss.get("hook") is not None:
                    ss["hook"]()
                init_ap, B_init = ss["init"]
                fin, B_fin = ST[ss["fin"]], B_ST[ss["fin"]]
                k.do(DVE, lambda h: h.scalar_tensor_tensor(out=TI_[p4][:, 0:T], in0=TI_[p4][:, 0:T], scalar=0.5,
                                                           in1=TR_[p4][:, 0:T], op0=ALU.mult, op1=ALU.mult),
                     reads=[B_TI[p4], B_TR[p4]], writes=[B_TI[p4]])
                k.do(DVE, lambda h: h.tensor_tensor_scan(out=H_[p2][:, 0:T], data0=A_[p4][:, 0:T], data1=TI_[p4][:, 0:T],
                                                         initial=init_ap[:, j:j + 1], op0=ALU.mult, op1=ALU.add),
                     reads=[B_A[p4], B_TI[p4], B_init], writes=[B_H[p2]])
                k.do(DVE, lambda h: h.tensor_copy(out=fin[:, j:j + 1], in_=H_[p2][:, T - 1:T]),
                     reads=[B_H[p2]], writes=[B_fin])

            seg_dma_done = set()
            seg_cmp_done = set()

            def seg_dma(seg_id):
                if seg_id is None or seg_id in seg_dma_done:
                    return
                seg_dma_done.add(seg_id)
                r, b = seg_id % NR, seg_id % NBD
                load_bd(bd[b], 0, seg_id, B_bd[b])
                seg_consts(seg_id, segv[r], segh[r], segc[r], sege[r], B_seg[r], phase=1)
                k.dma(SP, w5r[r], seg_w5[seg_id], writes=[B_seg[r]], key="sc_" + B_seg[r].name)

            def seg_cmp(seg_id):
                if seg_id is None or seg_id in seg_cmp_done:
                    return
                seg_dma(seg_id)
                seg_cmp_done.add(seg_id)
                r = seg_id % NR
                seg_consts(seg_id, segv[r], segh[r], segc[r], sege[r], B_seg[r], phase=2)
                mi_ = 1 if seg_id < 2 else 0
                k.do(DVE, lambda h: h.tensor_tensor(out=segcb[r], in0=w5r[r][:, :, 2], in1=bv[:, :, mi_], op=ALU.mult),
                     reads=[B_seg[r], B_bv], writes=[B_seg[r]])
                k.do(DVE, lambda h: h.tensor_tensor(out=segcb[r], in0=segcb[r], in1=bcv, op=ALU.add),
                     reads=[B_seg[r], B_cst], writes=[B_seg[r]])

            def seg_prep(seg_id):
                seg_cmp(seg_id)

            if pre_t0:
                emit_t0(pre_t0)
                flush_t0()
            if mid is not None:
                mid()
            seg_prep(subs[0]["seg"])
            for tick in range(NU + 8):
                if 0 <= tick - 3 < NU:
                    S2pe(tick - 3)
                if tick < NU:
                    ssi, j = units[tick]
                    if j == 0:
                        nxt = subs[ssi + 1]["seg"] if ssi + 1 < len(subs) else None
                        nxt2 = subs[ssi + 2]["seg"] if ssi + 2 < len(subs) else None
                        seg_prep(subs[ssi]["seg"])
                        seg_dma(nxt)
                        seg_dma(nxt2)
                        seg_cmp(nxt)
                    P1(tick)
                    if j % 2 == 0 and t0_pos[0] < len(t0_list) and t0_list[t0_pos[0]][0] <= ssi + 1:
                        emit_t0(1)
                if 0 <= tick - 1 < NU:
                    S1(tick - 1)
                u4 = tick - 4
                if u4 >= 0 and u4 % 4 == 3 and u4 < NU:
                    for uu in range(u4 - 3, u4 + 1):
                        SQ(uu)
                    for uu in range(u4 - 3, u4 + 1):
                        D2(uu)
                if 0 <= tick - 3 < NU:
                    S2(tick - 3)
                flush_t0()

        def fence(srcs, dsts):
            evs = []
            for b in srcs:
                if b.w is not None:
                    evs.append(b.w)
                evs.extend(b.r)
            for d_ in dsts:
                d_.r.extend(evs)

        FINE = lambda: ALLP + B_pV + B_pGa + B_pGx
        COARSE = lambda: B_tmp + [B_y, B_xbb] + [B_ps[n_] for n_ in "ABCD"]

        build_gbc()

        def boundary(s, prev_idx, dst, B_dst):
            Sp, Bp = ST[prev_idx], B_ST[prev_idx]
            k.do(DVE, lambda h: h.scalar_tensor_tensor(out=cf, in0=Sp, scalar=ohv[:, s:s + 1], in1=cf, op0=ALU.mult,
                                                       op1=ALU.add), reads=[Bp, B_cst, B_cf], writes=[B_cf])
            k.do(DVE, lambda h: h.tensor_tensor(out=Dt, in0=ctxr, in1=Sp, op=ALU.subtract),
                 reads=[B_ctxr, Bp], writes=[B_Dt])
            k.do(DVE, lambda h: h.scalar_tensor_tensor(out=dst, in0=Dt, scalar=ohv[:, s:s + 1], in1=Sp, op0=ALU.mult,
                                                       op1=ALU.add), reads=[B_Dt, B_cst, Bp], writes=[B_dst])

        def first_hook():
            k.do(DVE, lambda h: h.tensor_copy(out=ctxr, in_=ST[1]), reads=[B_ST[1]], writes=[B_ctxr])
            k.do(DVE, lambda h: h.tensor_copy(out=ST[3], in_=ST[0]), reads=[B_ST[0]], writes=[B_ST[3]])
            k.do(POOL, lambda h: h.memset(cf, 0.0), writes=[B_cf])
            boundary(0, 3, SI[0], B_SI[0])

        all_subs = [
            dict(src=ctx2[0], T=CTX, L=CTX, mi=1, seg=0, init=(zero8, B_z8), fin=0, hook=None),
            dict(src=ctx2[1], T=CTX, L=CTX, mi=1, seg=1, init=(zero8, B_z8), fin=1, hook=None),
        ]
        prev_fin = 3
        for s_ in range(7):
            f0 = (2 * s_) % 3
            f1 = (2 * s_ + 1) % 3
            if s_ == 0:
                hook = first_hook
            else:
                hook = (lambda s_=s_, pf=prev_fin: boundary(s_, pf, SI[s_ % 2], B_SI[s_ % 2]))
            all_subs.append(dict(src=x_out[s_, 0:512, :], T=512, L=64, mi=0, seg=2 + s_,
                                 init=(SI[s_ % 2], B_SI[s_ % 2]), fin=f0, hook=hook,
                                 pre=(lambda: wv_rescale(0)) if s_ == 0 else None))
            all_subs.append(dict(src=x_out[s_, 512:1024, :], T=512, L=64, mi=0, seg=2 + s_,
                                 init=(ST[f0], B_ST[f0]), fin=f1, hook=None))
            prev_fin = f1
        fence(COARSE(), FINE())
        run_pipeline(all_subs, pre_t0=4, mid=late_mod)
        fence(FINE(), COARSE())
        fence([B_gbc], [B_y])
        cr = SI[1]
        boundary(7, prev_fin, SI[1], B_SI[1])
        B_cr = B_SI[1]


        hlT = xnT
        B_hl = B_xh
        pb_prev = None
        pend_w = wload(wada_v, 16 * 256)
        for t in range(8):
            pb = t0_tile(x_own[t * 128:(t + 1) * 128, :], t * 128, hlT, B_xh, modulate=True, defer=True, pn="CD"[t % 2])
            if pb_prev is not None:
                pb_prev()
            pb_prev = pb
            cur_w = pend_w
            if t < 7:
                pend_w = wload(wada_v, (17 + t) * 256)
            mod_block(cur_w, 16 + t, pp="AB")
        pb_prev()
        k.do(DVE, lambda h: h.tensor_copy(out=gtv, in_=modv[:, 0, 32:48]), reads=[B_sm], writes=[B_cst])
        WB.append((xtw[0], B_xt[0], "wxt0"))
        T1, T2 = xt[1][:, 0:1024], xt[1][:, 1024:2048]
        B_T1, B_T2 = Buf("T1"), Buf("T2")
        fence([B_xt[1]], [B_T1, B_T2])
        load_bd(bdo, 0, NSEG, B_bdo)
        load_bd(bdo, 2, NSEG + 1, B_bdo)
        B_own = Buf("own")
        for d in range(2):
            seg_consts(NSEG + d, ownv[:, d], ownh[:, d], ownc[:, d], owne[:, d], B_own)
        k.dma(SP, w5o, seg_w5[NSEG], writes=[B_own], key="sc_own")

        def inproj(pn, cur, cc):
            mm_group(pn, TOK, lambda kc: WB[cur][0][:, kc, cc * 128:(cc + 1) * 128],
                     lambda kc, h0, hn: hlT[:, kc, h0:h0 + hn], [WB[cur][1], B_hlodd] + B_xh)

        def inproj1(pn, col0):
            cur = wload(win_v, col0, ncols=128)
            mm_group(pn, TOK, lambda kc: WB[cur][0][:, kc, 0:128],
                     lambda kc, h0, hn: hlT[:, kc, h0:h0 + hn], [WB[cur][1], B_hlodd] + B_xh)

        def gates_pe(j, slot, pa, pb):
            mm_group(pa, TOK, lambda kc: bdo[:, slot, j, :], lambda kc, h0, hn: xbb[:, h0:h0 + hn], [B_bdo, B_xbb] + bd_subs(B_bdo), nk=1)
            mm_group(pb, TOK, lambda kc: bdo[:, slot + 1, j, :], lambda kc, h0, hn: xbb[:, h0:h0 + hn], [B_bdo, B_xbb] + bd_subs(B_bdo), nk=1)

        def gates_act(j, d, pa, pb, ri, ii, ai):
            r, i_, a = tmp[ri], tmp[ii], tmp[ai]
            k.do(ACT, lambda h: h.activation(out=r[:], in_=PSM[pa][:], func=AF.Tanh, scale=0.5, bias=ownh[:, d, 0, j:j + 1]),
                 reads=[B_ps[pa], B_own], writes=[B_tmp[ri]])
            k.do(ACT, lambda h: h.activation(out=i_[:], in_=PSM[pb][:], func=AF.Tanh, scale=0.5, bias=ownh[:, d, 1, j:j + 1]),
                 reads=[B_ps[pb], B_own], writes=[B_tmp[ii]])
            k.do(ACT, lambda h: h.activation(out=a[:], in_=r[:], func=AF.Exp, scale=ownc[:, d, 0, j:j + 1],
                                             bias=ownc[:, d, 0, j:j + 1]), reads=[B_tmp[ri], B_own], writes=[B_tmp[ai]])
            k.do(ACT, lambda h: h.activation(out=r[:], in_=r[:], func=AF.Exp, scale=ownc[:, d, 1, j:j + 1],
                                             bias=ownc[:, d, 1, j:j + 1]), reads=[B_tmp[ri], B_own], writes=[B_tmp[ri]])

        def gates_sqrt(ri):
            r = tmp[ri]
            k.do(ACT, lambda h: h.activation(out=r[:], in_=r[:], func=AF.Sqrt, scale=-1.0, bias=ONE_AP),
                 reads=[B_tmp[ri], B_eps], writes=[B_tmp[ri]])

        def gates_dve(ri, ii, ai, init_ap, B_init, out_h, B_h, reverse):
            r, i_, a = tmp[ri], tmp[ii], tmp[ai]
            k.do(DVE, lambda h: h.scalar_tensor_tensor(out=i_[:], in0=i_[:], scalar=1.0, in1=tmp[1][:], op0=ALU.add,
                                                       op1=ALU.mult), reads=[B_tmp[ii], B_tmp[1]], writes=[B_tmp[ii]])
            k.do(DVE, lambda h: h.scalar_tensor_tensor(out=i_[:], in0=i_[:], scalar=0.5, in1=r[:], op0=ALU.mult,
                                                       op1=ALU.mult), reads=[B_tmp[ii], B_tmp[ri]], writes=[B_tmp[ii]])
            if not reverse:
                k.do(DVE, lambda h: h.tensor_tensor_scan(out=out_h[:], data0=a[:], data1=i_[:], initial=init_ap,
                                                         op0=ALU.mult, op1=ALU.add),
                     reads=[B_tmp[ai], B_tmp[ii], B_init], writes=[B_h])
            else:
                k.do(DVE, lambda h: h.tensor_tensor_scan(out=out_h[:, ::-1], data0=a[:, ::-1], data1=i_[:, ::-1],
                                                         initial=init_ap, op0=ALU.mult, op1=ALU.add),
                     reads=[B_tmp[ai], B_tmp[ii], B_init], writes=[B_h])

        v3 = lambda ap, a, b: ap.rearrange("p (r l) -> p r l", l=64)[:, :, a:b]

        def K1(jk):
            inproj1("B", 1024 + jk * 128)
            k.do(ACT, lambda h: h.activation(out=T1, in_=psB[:], func=AF.Copy), reads=[B_ps["B"]], writes=[B_T1])

        def K2(jk):
            inproj1("D", 2048 + jk * 128)
            k.do(DVE, lambda h: h.tensor_tensor(out=T1, in0=T1, in1=psD[:], op=ALU.mult),
                 reads=[B_ps["D"], B_T1], writes=[B_T1])
            k.do(DVE, lambda h: h.tensor_scalar(out=T2, in0=T1, scalar1=wcav[:, jk, 1:2], scalar2=None, op0=ALU.mult),
                 reads=[B_T1, B_cst], writes=[B_T2])
            k.do(DVE, lambda h: h.scalar_tensor_tensor(out=v3(T2, 1, 64), in0=v3(T1, 0, 63), scalar=wcav[:, jk, 0:1],
                                                       in1=v3(T2, 1, 64), op0=ALU.mult, op1=ALU.add),
                 reads=[B_T1, B_cst], writes=[B_T2])
            k.do(DVE, lambda h: h.scalar_tensor_tensor(out=v3(T2, 0, 63), in0=v3(T1, 1, 64), scalar=wcav[:, jk, 2:3],
                                                       in1=v3(T2, 0, 63), op0=ALU.mult, op1=ALU.add),
                 reads=[B_T1, B_cst], writes=[B_T2])

        def K3(jk):
            inproj1("B", 3072 + jk * 128)
            k.do(ACT, lambda h: h.activation(out=T1, in_=psB[:], func=AF.Silu), reads=[B_ps["B"]], writes=[B_T1])
            inproj1("D", 0 + jk * 128)

        def K4(jk):
            k.do(DVE, lambda h: h.tensor_tensor(out=T1, in0=T1, in1=psD[:], op=ALU.mult),
                 reads=[B_ps["D"], B_T1], writes=[B_T1])
            k.do(DVE, lambda h: h.tensor_tensor(out=yy[:, jk, :], in0=T1, in1=T2, op=ALU.mult),
                 reads=[B_T1, B_T2], writes=[B_y])

        for j in range(8):
            jk = j - 1
            inproj1("A", 4096 + j * 128)
            if jk >= 0:
                K1(jk)
            k.do(ACT, lambda h: h.activation(out=tmp[0][:], in_=psA[:], func=AF.Copy),
                 reads=[B_ps["A"]], writes=[B_tmp[0]])
            k.do(ACT, lambda h, j=j: h.activation(out=tmp[1][:], in_=psA[:], func=AF.Identity, scale=w5o[:, j, 2:3],
                                                  bias=bcv[:, j:j + 1]),
                 reads=[B_ps["A"], B_own, B_cst], writes=[B_tmp[1]])
            conv5(TOK, 64, j, w5o, B_own, tmp[0][:], tmp[1][:], B_tmp[0], B_tmp[1], skip_first=True)
            k.do(ACT, lambda h: h.activation(out=xbb[:, 0:1024], in_=tmp[1][:], func=AF.Copy),
                 reads=[B_tmp[1]], writes=[B_xbb])
            if jk >= 0:
                K2(jk)
            gates_pe(j, 0, "C", "D")
            gates_pe(j, 2, "B", "A")
            gates_act(j, 0, "C", "D", 2, 3, 4)
            inproj1("C", 5120 + j * 128)
            gates_act(j, 1, "B", "A", 5, 7, 0)
            gates_sqrt(2)
            gates_sqrt(5)
            if jk >= 0:
                K3(jk)
            gates_dve(2, 3, 4, cf[:, j:j + 1], B_cf, tmp[6], B_tmp[6], False)
            gates_dve(5, 7, 0, cr[:, j:j + 1], B_cr, tmp[8], B_tmp[8], True)
            k.do(DVE, lambda h: h.tensor_tensor(out=tmp[6][:], in0=tmp[6][:], in1=tmp[8][:], op=ALU.add),
                 reads=[B_tmp[8]], writes=[B_tmp[6]])
            if jk >= 0:
                K4(jk)
            k.do(ACT, lambda h: h.activation(out=tmp[2][:], in_=psC[:], func=AF.Silu),
                 reads=[B_ps["C"]], writes=[B_tmp[2]])
            k.do(DVE, lambda h, j=j: h.tensor_tensor(out=yy[:, 8 + j, :], in0=tmp[6][:], in1=tmp[2][:], op=ALU.mult),
                 reads=[B_tmp[6], B_tmp[2]], writes=[B_y])
        K1(7); K2(7); K3(7); K4(7)
        fence([B_T1, B_T2], [B_xt[1]])
        WB.append((xtw[1], B_xt[1], "wxt1"))

        fg, B_fg = tmp[8], B_tmp[8]
        k.dma(SP, fg[:], fgbc[:, 0:1024], writes=[B_fg], key="fg")
        fg2, B_fg2 = tmp[6], B_tmp[6]
        k.dma(SP, fg2[:], fgbc[:, 1024:2048], writes=[B_fg2], key="fg2")
        osbs, B_osbs = [tmp[0], tmp[1]], [B_tmp[0], B_tmp[1]]
        del WB[2:]
        for blk in range(8):
            cur = wload(wout_v, blk * 256)
            for cc in range(2):
                dch = blk * 2 + cc
                pn = "AB"[dch % 2]
                osb, B_osb = osbs[dch % 2], B_osbs[dch % 2]
                mm_group(pn, TOK, lambda kc, cur=cur, cc=cc: WB[cur][0][:, kc, cc * 128:(cc + 1) * 128],
                         lambda kc, h0, hn: yy[:, kc, h0:h0 + hn], [WB[cur][1], B_y])
                k.do(ACT, lambda h, pn=pn, dch=dch, osb=osb: h.activation(out=osb[:], in_=PSM[pn][:], func=AF.Identity,
                                                                          scale=gtv[:, dch:dch + 1]),
                     reads=[B_ps[pn], B_cst], writes=[B_osb])
                pt = "CD"[dch % 2]

                def emit(h, pt=pt, osb=osb):
                    for t in range(8):
                        ins = h.transpose(out=PSM[pt][:, t * 128:(t + 1) * 128], in_=osb[:, t * 128:(t + 1) * 128],
                                          identity=identf[:])
                    return ins
                pe_group(emit, [B_osb, B_id], [B_ps[pt]])
                k.do(DVE, lambda h, pt=pt, dch=dch: h.tensor_copy(
                    out=newlat[:, :, dch * 128:(dch + 1) * 128],
                    in_=PSM[pt][:, :].rearrange("p (t d) -> p t d", t=8)),
                    reads=[B_ps[pt]], writes=[B_wv, B_hlodd] + B_xh)
        for t in range(8):
            i = xcount[0] % 2
            xcount[0] += 1
            q = qcount[0] % NQ
            qcount[0] += 1
            k.dma(SP, xt[i][:], x_own[t * 128:(t + 1) * 128, :], writes=[B_xt[i]], key=f"xt{i}")
            k.do(DVE, lambda h, i=i, t=t: h.tensor_tensor(out=xt[i][:], in0=xt[i][:], in1=newlat[:, t, :], op=ALU.add),
                 reads=[B_wv] + B_xh, writes=[B_xt[i]])
            k.do(ACT, lambda h, i=i, q=q: h.activation(out=tmp[7][:, 0:1024].bitcast(BF16), in_=xt[i][:], func=AF.Square,
                                                       accum_out=ssq[q]),
                 reads=[B_xt[i]], writes=[B_tmp[7], B_q[q]])
            k.do(ACT, lambda h, q=q: h.activation(out=rstd[q], in_=ssq[q], func=AF.Sqrt, scale=1.0 / D, bias=EPS_AP),
                 reads=[B_q[q], B_eps], writes=[B_q[q]])
            k.do(DVE, lambda h, q=q: h.reciprocal(out=rstd[q], in_=rstd[q]), reads=[B_q[q]], writes=[B_q[q]])
            k.do(DVE, lambda h, i=i, q=q: h.scalar_tensor_tensor(out=xt[i][:, 0:1024], in0=xt[i][:, 0:1024], scalar=rstd[q],
                                                                 in1=fg[:], op0=ALU.mult, op1=ALU.mult),
                 reads=[B_q[q], B_fg], writes=[B_xt[i]])
            k.do(DVE, lambda h, i=i, q=q: h.scalar_tensor_tensor(out=xt[i][:, 1024:2048], in0=xt[i][:, 1024:2048],
                                                                 scalar=rstd[q], in1=fg2[:], op0=ALU.mult, op1=ALU.mult),
                 reads=[B_q[q], B_fg2], writes=[B_xt[i]])
            k.dma(SP, out_d[t * 128:(t + 1) * 128, :], xt[i][:], reads=[B_xt[i]], key=f"st{i}")
        for key in ("st0", "st1"):
            ds = k.dsems[key]
            SP.h.wait_ge(ds[0], ds[1])
    return nc


def kernel(x, c, ctx, c_ctx, norm_g, w_ada, b_ada, w_in, w_conv_a, w_conv_b, b_conv_b,
           lru_wa, lru_ba, lru_wx, lru_bx, lru_lambda, w_out, final_g):
    f = np.float32
    x = np.asarray(x, f)[0]
    ctx_ = np.asarray(ctx, f)[0]
    pm = lambda v, n: np.ascontiguousarray(np.asarray(v, f).reshape(n, 128).T)
    svec = np.ascontiguousarray(np.stack([pm(np.asarray(c)[0], 16), pm(c_ctx, 16)], axis=-1))
    gvec = pm(np.asarray(norm_g)[0], 16)
    bada = pm(np.asarray(b_ada)[0], 48)
    wcb = np.asarray(w_conv_b, f)[0]
    z = np.zeros_like(wcb[0])
    w5_nat = np.stack([wcb[0], wcb[1], wcb[2], wcb[3], z], 0)
    w5_rev = np.stack([z, wcb[3], wcb[2], wcb[1], wcb[0]], 0)
    lay5 = lambda w5: np.ascontiguousarray(w5.reshape(5, 8, 128).transpose(2, 1, 0))
    wca = np.ascontiguousarray(np.asarray(w_conv_a, f)[0].reshape(3, 8, 128).transpose(2, 1, 0))
    bconv = pm(np.asarray(b_conv_b)[0], 8)
    wa = np.asarray(lru_wa, f)[0]; wx = np.asarray(lru_wx, f)[0]
    ba = np.asarray(lru_ba, f)[0]; bx = np.asarray(lru_bx, f)[0]; lam = np.asarray(lru_lambda, f)[0]
    vec = lambda d: np.ascontiguousarray(np.stack([pm(ba[d], 8), pm(bx[d], 8), pm(lam[d], 8)], 1))
    fgbc = np.ascontiguousarray(np.broadcast_to(np.asarray(final_g, f)[None, :], (128, D)))
    eye = np.eye(128, dtype=f)
    ctx2 = np.ascontiguousarray(np.stack([ctx_, ctx_[::-1]], 0))
    xb = x.reshape(8, TOK, D)
    in_maps = []
    for kk in range(NCORES):
        dirs = [0, 1]
        revs = [False, True]
        blocks = []
        for s in range(7):
            if s < kk:
                dirs.append(0); revs.append(False); blocks.append(xb[s])
            else:
                dirs.append(1); revs.append(True); blocks.append(xb[7 - (s - kk)][::-1])
        dirs += [0, 1]; revs += [False, False]
        in_maps.append({
            "x_own": np.ascontiguousarray(xb[kk]),
            "x_out": np.ascontiguousarray(np.stack(blocks, 0)),
            "ctx2": ctx2, "svec": svec, "gvec": gvec,
            "w_ada": np.asarray(w_ada, f)[0], "b_ada": bada,
            "w_in": np.asarray(w_in, f)[0], "w_out": np.asarray(w_out, f)[0],
            "seg_wa": np.ascontiguousarray(np.stack([wa[d] for d in dirs], 0)),
            "seg_wx": np.ascontiguousarray(np.stack([wx[d] for d in dirs], 0)),
            "seg_vec": np.ascontiguousarray(np.stack([vec(d) for d in dirs], 0)),
            "seg_w5": np.ascontiguousarray(np.stack([lay5(w5_rev if r else w5_nat) for r in revs], 0)),
            "bconv": bconv, "wca": wca, "fgbc": fgbc,
            "onehot": np.ascontiguousarray(np.broadcast_to((np.arange(8) == kk).astype(f)[None, :], (128, 8))),
            "eye": eye,
        })
    nc = build_program()
    res = run_bass_kernel_spmd(nc, in_maps, core_ids=list(range(NCORES)))
    out = np.concatenate([res.results[r]["out"] for r in range(NCORES)], axis=0)
    return out.reshape(1, NCORES * TOK, D).astype(np.float32)
```
